# Optimizing a Trainium2 kernel written in Bass

```python
import jax, jax.numpy as jnp
from jax import lax
import numpy as np

D_MODEL = 1024
BATCH = 8
SEQ = 2048
DEPTH = 2

CHUNK = 128
RET_HEADS = 4
RET_QK_DIM = 128
RET_V_DIM = 256
RET_QK_WIDTH = RET_HEADS * RET_QK_DIM
RET_V_WIDTH = RET_HEADS * RET_V_DIM
GMLP_GROUPS = 4
GMLP_WIDTH = D_MODEL
GMLP_GROUP_DIM = GMLP_WIDTH // GMLP_GROUPS
POOL_WINDOWS = (2, 4, 8, 16)
POOL_GROUPS = 4
POOL_WIDTH = D_MODEL
POOL_GROUP_DIM = POOL_WIDTH // POOL_GROUPS
N_BRANCH = 3
BRANCH_WIDTH = D_MODEL
D_FF = 4 * D_MODEL
N_MOD = 6
ROPE_BASE = 10000.0
EPS = 1e-6
SPLITS = (RET_QK_WIDTH, RET_QK_WIDTH, RET_V_WIDTH, RET_V_WIDTH,
          GMLP_WIDTH, GMLP_WIDTH, POOL_WIDTH, N_BRANCH * D_MODEL)
D_IN = 2 * RET_QK_WIDTH + 2 * RET_V_WIDTH + 2 * GMLP_WIDTH + POOL_WIDTH + N_BRANCH * D_MODEL

kernel_name = "hybrid_retention_gmlp_pool_adaln"


def rms_norm(x, gain=None):
    xf = x.astype(jnp.float32)
    y = xf * lax.rsqrt(jnp.mean(xf * xf, axis=-1, keepdims=True) + EPS)
    if gain is not None:
        y = y * gain.astype(jnp.float32)
    return y.astype(x.dtype)


def split_columns(proj):
    pieces, start = [], 0
    for w in SPLITS:
        pieces.append(proj[..., start:start + w])
        start += w
    return pieces


def rotary(x, positions):
    half = x.shape[-1] // 2
    inv_freq = ROPE_BASE ** (-jnp.arange(half, dtype=jnp.float32) / half)
    ang = positions.astype(jnp.float32)[:, :, None] * inv_freq
    cos = jnp.cos(ang)[:, :, None, :]
    sin = jnp.sin(ang)[:, :, None, :]
    xf = x.astype(jnp.float32)
    x1, x2 = xf[..., :half], xf[..., half:]
    return jnp.concatenate([x1 * cos - x2 * sin, x2 * cos + x1 * sin], axis=-1).astype(x.dtype)


def retention(q, k, v, g, positions):
    B, S, _ = q.shape
    nc = S // CHUNK
    dt = q.dtype
    q = rotary(q.reshape(B, S, RET_HEADS, RET_QK_DIM), positions)
    k = rotary(k.reshape(B, S, RET_HEADS, RET_QK_DIM), positions) * (RET_QK_DIM ** -0.5)
    v = v.reshape(B, S, RET_HEADS, RET_V_DIM)
    log_gamma = jnp.log1p(-jnp.power(2.0, -5.0 - jnp.arange(RET_HEADS, dtype=jnp.float32)))
    pos = jnp.arange(CHUNK, dtype=jnp.float32)
    rel = pos[:, None] - pos[None, :]
    causal = rel >= 0
    decay_intra = jnp.where(causal[None],
                            jnp.exp(log_gamma[:, None, None] * jnp.where(causal, rel, 0.0)[None]),
                            0.0)
    decay_q = jnp.exp(log_gamma[:, None] * (pos + 1.0)[None])
    decay_k = jnp.exp(log_gamma[:, None] * (CHUNK - 1.0 - pos)[None])
    decay_chunk = jnp.exp(log_gamma * CHUNK)

    qc = q.reshape(B, nc, CHUNK, RET_HEADS, RET_QK_DIM)
    kc = k.reshape(B, nc, CHUNK, RET_HEADS, RET_QK_DIM)
    vc = v.reshape(B, nc, CHUNK, RET_HEADS, RET_V_DIM)
    scores = jnp.einsum('bnihd,bnjhd->bnhij', qc, kc) * decay_intra.astype(dt)
    intra = jnp.einsum('bnhij,bnjhe->bnihe', scores, vc)
    kv = jnp.einsum('bnjhd,hj,bnjhe->nbhde', kc, decay_k.astype(dt), vc).astype(jnp.float32)

    def step(state, kv_n):
        return decay_chunk[None, :, None, None] * state + kv_n, state

    _, prev = lax.scan(step, jnp.zeros((B, RET_HEADS, RET_QK_DIM, RET_V_DIM), jnp.float32), kv)
    cross = jnp.einsum('bnihd,hi,nbhde->bnihe', qc, decay_q.astype(dt), prev.astype(dt))
    o = (intra + cross).reshape(B, S, RET_HEADS, RET_V_DIM)
    o = rms_norm(o).reshape(B, S, RET_V_WIDTH)
    return jax.nn.silu(g) * o


def spatial_gating(u, v, w_s, b_s, v_gain):
    B, S, _ = u.shape
    nc = S // CHUNK
    u = jax.nn.gelu(u)
    v = rms_norm(jax.nn.gelu(v), v_gain)
    vc = v.reshape(B, nc, CHUNK, GMLP_GROUPS, GMLP_GROUP_DIM)
    mask = jnp.tril(jnp.ones((CHUNK, CHUNK), dtype=bool))
    w = jnp.where(mask[None], w_s, jnp.zeros_like(w_s))
    mixed = jnp.einsum('gts,bnsgc->bntgc', w, vc) + b_s.T[None, None, :, :, None]
    return u * mixed.reshape(B, S, GMLP_WIDTH)


def multiscale_pool(p, w_pool, b_pool, scale):
    B, S, _ = p.shape
    pg = p.reshape(B, S, POOL_GROUPS, POOL_GROUP_DIM).astype(jnp.float32)
    cs = jnp.concatenate([jnp.zeros((B, 1, POOL_GROUPS, POOL_GROUP_DIM), jnp.float32),
                          jnp.cumsum(pg, axis=1)], axis=1)
    t = jnp.arange(1, S + 1)
    outs = []
    for gi, w in enumerate(POOL_WINDOWS):
        start = jnp.maximum(t - w, 0)
        window_sum = cs[:, 1:, gi] - cs[:, start, gi]
        count = jnp.minimum(t, w).astype(jnp.float32)[None, :, None]
        outs.append(window_sum / count - pg[:, :, gi])
    pooled = jnp.stack(outs, axis=2).astype(p.dtype)
    mixed = jnp.einsum('bsgc,gce->bsge', pooled, w_pool) + b_pool[None, None]
    return mixed.reshape(B, S, POOL_WIDTH) * scale


def setup_inputs(seed: int = 0) -> dict:
    key = jax.random.key(seed)
    ks = jax.random.split(key, 20)
    f32 = jnp.float32
    nrm = lambda k, shape, s: jax.random.normal(k, shape, f32) * s
    return {
        "x": nrm(ks[0], (BATCH, SEQ, D_MODEL), 1.0),
        "c": nrm(ks[1], (BATCH, D_MODEL), 1.0),
        "positions": jnp.broadcast_to(jnp.arange(SEQ, dtype=jnp.int32), (BATCH, SEQ)),
        "w_ada": nrm(ks[2], (DEPTH, D_MODEL, N_MOD * D_MODEL), 0.5 * D_MODEL ** -0.5),
        "b_ada": nrm(ks[3], (DEPTH, N_MOD * D_MODEL), 0.01),
        "norm1": 1.0 + nrm(ks[4], (DEPTH, D_MODEL), 0.02),
        "norm2": 1.0 + nrm(ks[5], (DEPTH, D_MODEL), 0.02),
        "w_in": nrm(ks[6], (DEPTH, D_MODEL, D_IN), D_MODEL ** -0.5),
        "ws_gmlp": nrm(ks[7], (DEPTH, GMLP_GROUPS, CHUNK, CHUNK), CHUNK ** -0.5),
        "bs_gmlp": 1.0 + nrm(ks[8], (DEPTH, GMLP_GROUPS, CHUNK), 0.1),
        "vnorm_gmlp": 1.0 + nrm(ks[9], (DEPTH, GMLP_WIDTH), 0.02),
        "w_pool": nrm(ks[10], (DEPTH, POOL_GROUPS, POOL_GROUP_DIM, POOL_GROUP_DIM), POOL_GROUP_DIM ** -0.5),
        "b_pool": nrm(ks[11], (DEPTH, POOL_GROUPS, POOL_GROUP_DIM), 0.01),
        "pool_scale": 1.0 + nrm(ks[12], (DEPTH, POOL_WIDTH), 0.1),
        "w_branch": nrm(ks[13], (DEPTH, N_BRANCH, BRANCH_WIDTH, D_MODEL), BRANCH_WIDTH ** -0.5),
        "w_out": nrm(ks[14], (DEPTH, D_MODEL, D_MODEL), D_MODEL ** -0.5),
        "w_ff1": nrm(ks[15], (DEPTH, D_MODEL, D_FF), D_MODEL ** -0.5),
        "w_ff2": nrm(ks[16], (DEPTH, D_FF, D_MODEL), D_FF ** -0.5),
        "final_norm": 1.0 + nrm(ks[17], (D_MODEL,), 0.02),
    }


def reference(x, c, positions, w_ada, b_ada, norm1, norm2, w_in, ws_gmlp, bs_gmlp, vnorm_gmlp,
              w_pool, b_pool, pool_scale, w_branch, w_out, w_ff1, w_ff2, final_norm):
    B, S, _ = x.shape
    c_act = jax.nn.silu(c)
    for l in range(DEPTH):
        mod = c_act @ w_ada[l] + b_ada[l]
        sh1, sc1, gt1, sh2, sc2, gt2 = [m[:, None, :] for m in jnp.split(mod, N_MOD, axis=-1)]

        h = rms_norm(x, norm1[l]) * (1.0 + sc1) + sh1
        proj = h @ w_in[l]
        q, k, v, g, u, vs, p, gate_cols = split_columns(proj)
        y_ret = retention(q, k, v, g, positions)
        y_sgu = spatial_gating(u, vs, ws_gmlp[l], bs_gmlp[l], vnorm_gmlp[l])
        y_pool = multiscale_pool(p, w_pool[l], b_pool[l], pool_scale[l])
        branches = jnp.stack([y_ret, y_sgu, y_pool], axis=2)
        branch_proj = jnp.einsum('bsnc,ncd->bsnd', branches, w_branch[l])
        gates = jax.nn.sigmoid(gate_cols.reshape(B, S, N_BRANCH, D_MODEL))
        merged = jnp.sum(gates * branch_proj, axis=2)
        x = x + gt1 * (merged @ w_out[l])

        h2 = rms_norm(x, norm2[l]) * (1.0 + sc2) + sh2
        hidden = jnp.square(jax.nn.relu(h2 @ w_ff1[l]))
        x = x + gt2 * (hidden @ w_ff2[l])
    return rms_norm(x, final_norm)
```

```python
import contextlib
import numpy as np
import ml_dtypes
import concourse.bass as bass
import concourse.mybir as mybir
from concourse.bass_utils import run_bass_kernel_spmd

F32 = mybir.dt.float32
BF16 = mybir.dt.bfloat16
I32 = mybir.dt.int32
AF = mybir.ActivationFunctionType
ALU = mybir.AluOpType

D = 1024
SEQ = 2048
NB = 8
TG = 1024
NG = SEQ // TG
NCH = TG // 128
DIN = 9216
DFF = 4096
EPS = 1e-6
OFF_Q, OFF_K, OFF_V, OFF_G, OFF_U, OFF_VS, OFF_P, OFF_GATE = 0, 512, 1024, 2048, 3072, 4096, 5120, 6144
POOL_W = (2, 4, 8, 16)
ENG = ("pe", "act", "dve", "pool", "sp")


class Buf:
    __slots__ = ("name", "writer", "readers", "excl")

    def __init__(self, name, excl=False):
        self.name = name
        self.writer = None
        self.readers = []
        self.excl = excl


class Sched:
    def __init__(self, nc, stack, same_engine_sync=True):
        self.nc = nc
        self.stack = stack
        self.sem = {e: stack.enter_context(nc.semaphore("prog_" + e)) for e in ENG}
        self.cnt = {e: 0 for e in ENG}
        self.seen = {e: {} for e in ENG}
        self.same = same_engine_sync
        self.dsem = {}
        self.prog = {e: [] for e in ENG}

    def _semh(self, kind, key):
        return self.sem[key] if kind == "e" else self.dsem[key][0]

    def _need(self, eng, deps, tok):
        if tok is None:
            return
        kind, key, count = tok
        if kind == "e" and key == eng:
            if not self.same or eng in ("pe", "sp", "pool"):
                return
        k = (kind, key)
        if deps.get(k, 0) < count:
            deps[k] = count

    def _waits(self, eng, deps):
        out = []
        for (kind, key), count in deps.items():
            if self.seen[eng].get((kind, key), 0) >= count:
                continue
            out.append((self._semh(kind, key), count))
            self.seen[eng][(kind, key)] = count
        return out

    def _deps(self, eng, reads, writes):
        deps = {}
        for b in reads:
            self._need(eng, deps, b.writer)
            if b.excl:
                for r in b.readers:
                    if not (r[0] == "e" and r[1] == eng):
                        self._need(eng, deps, r)
        for b in writes:
            self._need(eng, deps, b.writer)
            for r in b.readers:
                self._need(eng, deps, r)
        return deps

    def _mark(self, tok, reads, writes):
        for b in reads:
            b.readers.append(tok)
            if len(b.readers) > 64:
                best = {}
                for (k, key, c) in b.readers:
                    if best.get((k, key), 0) < c:
                        best[(k, key)] = c
                b.readers = [(k, key, c) for (k, key), c in best.items()]
        for b in writes:
            b.writer = tok
            b.readers = []

    def op(self, eng, fn, reads=(), writes=(), inc=True):
        waits = self._waits(eng, self._deps(eng, reads, writes))
        if inc:
            self.cnt[eng] += 1
            tok = ("e", eng, self.cnt[eng])
            self.prog[eng].append((waits, fn, (self.sem[eng], 1)))
        else:
            tok = ("e", eng, self.cnt[eng] + 1)
            self.prog[eng].append((waits, fn, None))
        self._mark(tok, reads, writes)

    def dma(self, q, semname, pairs, reads=(), writes=()):
        if semname not in self.dsem:
            self.dsem[semname] = [self.stack.enter_context(self.nc.semaphore("d_" + semname)), 0]
        waits = self._waits(q, self._deps(q, reads, writes))
        ent = self.dsem[semname]
        for i, (o, a) in enumerate(pairs):
            ent[1] += 16
            self.prog[q].append((waits if i == 0 else [],
                                 (lambda e, o=o, a=a: e.dma_start(out=o, in_=a)), (ent[0], 16)))
        tok = ("d", semname, ent[1])
        self._mark(tok, reads, writes)

    def inherit(self, new_bufs, old_bufs):
        toks = []
        for b in old_bufs:
            if b.writer is not None:
                toks.append(b.writer)
            toks.extend(b.readers)
        for b in new_bufs:
            b.readers = list(toks)

    def final_wait(self, eng, bufs):
        deps = {}
        for b in bufs:
            self._need(eng, deps, b.writer)
            for r in b.readers:
                self._need(eng, deps, r)
        w = self._waits(eng, deps)
        if w:
            self.prog[eng].append((w, None, None))

    @staticmethod
    def _replay(prog, e):
        for waits, fn, inc in prog:
            for (s, c) in waits:
                e.wait_ge(s, c)
            if fn is not None:
                ins = fn(e)
                if inc is not None:
                    ins.then_inc(inc[0], inc[1])

    def emit(self):
        with self.nc.Block() as block:
            @block.tensor
            def _(e):
                Sched._replay(self.prog["pe"], e)

            @block.scalar
            def _(e):
                Sched._replay(self.prog["act"], e)

            @block.vector
            def _(e):
                Sched._replay(self.prog["dve"], e)

            @block.gpsimd
            def _(e):
                Sched._replay(self.prog["pool"], e)

            @block.sync
            def _(e):
                Sched._replay(self.prog["sp"], e)


def _consts():
    c = {}
    gam = 1.0 - 2.0 ** (-5.0 - np.arange(4, dtype=np.float64))
    lg = np.log(gam)
    s = 128.0 ** -0.5
    j = np.arange(128)[:, None]
    i = np.arange(128)[None, :]
    mT = np.zeros((128, 4, 128), np.float64)
    for h in range(4):
        mT[:, h, :] = np.where(i >= j, np.exp(lg[h] * np.maximum(i - j, 0)), 0.0) * s
    c["maskT"] = mT.astype(np.float32)
    dq = np.zeros((128, 4, 128), np.float64)
    for h in range(4):
        dq[:, h, :] = np.exp(lg[h] * (np.arange(128) + 1.0))[None, :]
    c["dq"] = dq.astype(np.float32)
    dk = np.zeros((128, 4), np.float64)
    for h in range(4):
        dk[:, h] = np.exp(lg[h] * (127.0 - np.arange(128))) * s
    c["dk"] = dk.astype(np.float32)
    c["dchunk"] = [float(np.float32(np.exp(lg[h] * 128.0))) for h in range(4)]
    half = 64
    invf = (10000.0 ** (-np.arange(half, dtype=np.float32) / half)).astype(np.float32)
    c["invf"] = np.broadcast_to(invf[None, :], (128, 64)).copy()
    c["tril"] = np.tril(np.ones((128, 128), np.float32))
    c["identf"] = np.eye(128, dtype=np.float32)
    c["identb"] = np.eye(128, dtype=np.float32).astype(ml_dtypes.bfloat16)
    c["onesd"] = np.full((128, 128), 1.0 / D, np.float32).astype(ml_dtypes.bfloat16)
    pm = np.zeros((128, 3, 4, 128), np.float32)
    invc = np.zeros((128, 4, 128), np.float32)
    sidx = np.arange(128)[:, None]
    tidx = np.arange(128)[None, :]
    for wi, w in enumerate(POOL_W):
        cur = ((sidx <= tidx) & (sidx > tidx - w)).astype(np.float32)
        cur = cur - w * (sidx == tidx)
        pm[:, 0, wi, :] = cur
        pm[:, 1, wi, :] = (sidx - 128 > tidx - w).astype(np.float32)
        cnt = np.minimum(tidx + 1, w).astype(np.float32)
        first = ((sidx <= tidx) & (sidx > tidx - w)).astype(np.float32) - cnt * (sidx == tidx)
        pm[:, 2, wi, :] = first
        invc[:, wi, :] = 1.0 / cnt
    c["poolm"] = pm.astype(ml_dtypes.bfloat16)
    c["invc"] = invc
    return c


def build_program():
    nc = bass.Bass("TRN2", target_bir_lowering=False)
    K = _consts()

    def din(name, shape, dt=F32):
        return nc.dram_tensor(name, list(shape), dt, kind="ExternalInput").ap()

    x_d = din("x", [SEQ, D])
    c_d = din("c_col", [128, 8])
    pos_d = din("pos", [128, 16], I32)
    wada_d = din("w_ada", [2, D, 6 * D])
    bada_d = din("b_adaT", [128, 2, 48])
    n1_d = din("norm1T", [128, 2, 8])
    n2_d = din("norm2T", [128, 2, 8])
    fn_d = din("fnT", [128, 8])
    win_d = din("w_in", [2, D, DIN])
    ws_d = din("ws_gmlp", [2, 4, 128, 128])
    bs_d = din("bsT", [128, 2, 4])
    vg_d = din("vg_bc", [2, 128, 1024])
    wpool_d = din("w_pool", [2, 4, 256, 256])
    bp_d = din("b_poolT", [128, 2, 8])
    psc_d = din("pscaleT", [128, 2, 8])
    wbr_d = din("w_branch", [2, 3, D, D])
    wout_d = din("w_out", [2, D, D])
    wff1_d = din("w_ff1", [2, D, DFF])
    wff2_d = din("w_ff2", [2, DFF, D])
    maskT_d = din("maskT", [128, 4, 128])
    dq_d = din("dq", [128, 4, 128])
    dk_d = din("dk", [128, 4])
    invf_d = din("invf", [128, 64])
    tril_d = din("tril", [128, 128])
    identf_d = din("identf", [128, 128])
    identb_d = din("identb", [128, 128], BF16)
    onesd_d = din("onesd", [128, 128], BF16)
    poolm_d = din("poolm", [128, 3, 4, 128], BF16)
    invc_d = din("invc", [128, 4, 128])
    out_d = nc.dram_tensor("out", [SEQ, D], F32, kind="ExternalOutput").ap()

    with contextlib.ExitStack() as st:
        S = Sched(nc, st)

        def sb(name, shape, dt=F32):
            return st.enter_context(nc.sbuf_tensor(name, list(shape), dt))

        xT = sb("xT", [128, 8, TG])
        XB = [[Buf(f"x{k}{h}") for h in range(2)] for k in range(8)]
        hT = sb("hT", [128, 8, TG], BF16)
        HB = [[Buf(f"h{k}{h}") for h in range(2)] for k in range(8)]
        arena = sb("arena", [128, 24576], BF16)
        AR = [Buf(f"ar{i}") for i in range(24)]
        merged = arena[:, 0:16384].bitcast(F32).rearrange("p (k t) -> p k t", k=8)
        ynT = arena[:, 16384:24576].rearrange("p (k t) -> p k t", k=8)
        gv = arena[:, 0:8192].rearrange("p (c f) -> p c f", c=8)
        hid = [arena[:, 0:8192].rearrange("p (k t) -> p k t", k=8),
               arena[:, 8192:16384].rearrange("p (k t) -> p k t", k=8)]

        def MB(kd, half):
            return AR[2 * kd + half]

        def YB(kc):
            return AR[16 + kc]

        NSLOT = 4
        wslot = [sb(f"wslot{i}", [128, 4096], BF16) for i in range(NSLOT)]
        WS = [Buf(f"ws{i}") for i in range(NSLOT)]
        wctr = [0]

        cos_t = sb("cos_t", [128, 16, 64]); sin_t = sb("sin_t", [128, 16, 64]); CS = Buf("cs")
        Sst = sb("Sst", [128, 2, 4, 256]); Sbf = sb("Sbf", [128, 2, 4, 256], BF16)
        SB_ = [[Buf(f"S{l}{h}") for h in range(4)] for l in range(2)]
        SBF = [[Buf(f"Sbf{l}{h}") for h in range(4)] for l in range(2)]
        modT = sb("modT", [128, 2, 48]); MOD = [Buf("mod0"), Buf("mod1")]
        g12 = sb("g12", [128, 2, 2, 8])
        n1T = sb("n1T", [128, 2, 8]); n2T = sb("n2T", [128, 2, 8]); fnT = sb("fnT_s", [128, 8])
        badaT = sb("badaT", [128, 2, 48])
        bsT = sb("bsT_s", [128, 2, 4]); bpT = sb("bpT", [128, 2, 8]); pscT = sb("pscT", [128, 2, 8])
        SMALL = Buf("small")
        bps = sb("bps", [128, 2, 8])
        vg = sb("vg", [128, 2, 1024]); VG = Buf("vg")
        maskT = sb("maskT_s", [128, 4, 128]); dq = sb("dq_s", [128, 4, 128]); dk = sb("dk_s", [128, 4])
        invc = sb("invc_s", [128, 4, 128])
        tril = sb("tril_s", [128, 128])
        identf = sb("identf_s", [128, 128]); identb = sb("identb_s", [128, 128], BF16)
        onesd = sb("onesd_s", [128, 128], BF16)
        poolm = sb("poolm_s", [128, 3, 4, 128], BF16)
        CONST = Buf("const")
        WsT = sb("WsT", [128, 2, 4, 128], BF16); WST = Buf("wst")
        wp = sb("wp", [128, 4, 2, 256], BF16); WP = Buf("wp")
        halo = sb("halo", [128, 2, 1024], BF16); HALO = [Buf("halo0"), Buf("halo1")]
        eps_t = sb("eps_t", [128, 1])
        cact = sb("cact", [128, 8], BF16); CACT = Buf("cact")
        temp = sb("temp", [128, 14592], BF16)
        tstate = {"bufs": []}

        ps_t = [st.enter_context(nc.psum_tensor(f"psb{i}", [128, 512], F32)) for i in range(8)]
        PB = [Buf(f"pb{i}", excl=True) for i in range(8)]

        def temp_phase(spec):
            off = 0
            views = {}
            bufs = {}
            newb = []
            for name, shape, dt in spec:
                n = int(np.prod(shape[1:]))
                nb16 = n * (2 if dt in (F32, I32) else 1)
                v = temp[:, off:off + nb16]
                if dt != BF16:
                    v = v.bitcast(dt)
                if len(shape) == 3:
                    v = v.rearrange("p (a b) -> p a b", a=shape[1])
                elif len(shape) == 4:
                    v = v.rearrange("p (a b c) -> p a b c", a=shape[1], b=shape[2])
                views[name] = v
                bufs[name] = Buf("t_" + name)
                newb.append(bufs[name])
                off += nb16
            assert off <= 14592, off
            S.inherit(newb, tstate["bufs"])
            tstate["bufs"] = newb
            return views, bufs

        def mm_group(out_ap, pairs, reads, writes):
            n = len(pairs)
            for i, (l, r) in enumerate(pairs):
                S.op("pe", (lambda e, l=l, r=r, i=i: e.matmul(out_ap, lhsT=l, rhs=r, start=(i == 0), stop=(i == n - 1))),
                     reads=reads, writes=writes, inc=(i == n - 1))

        def wload(parts, kd, width):
            s = wctr[0] % NSLOT
            wctr[0] += 1
            view = wslot[s][:, 0:kd * width].rearrange("p (k w) -> p k w", k=kd)
            S.dma("pool", f"w{s}", [(view[:, :, c0:c0 + w], ap) for (c0, w, ap) in parts], writes=[WS[s]])
            return view, WS[s]

        def wcols(dram2d, c0, w):
            return dram2d[:, c0:c0 + w].rearrange("(k p) w -> p k w", p=128)

        def act(out, in_, func, reads, writes, bias=None, scale=None, accum=None):
            kw = {}
            if bias is not None:
                kw["bias"] = bias
            if scale is not None:
                kw["scale"] = scale
            if accum is not None:
                kw["accum_out"] = accum
            S.op("act", (lambda e: e.activation(out, in_, func, **kw)), reads=reads, writes=writes)

        def tt(out, a, b, op, reads, writes, eng="dve"):
            S.op(eng, (lambda e: e.tensor_tensor(out, a, b, op=op)), reads=reads, writes=writes)

        def tsc(out, a, s1, s2, op0, op1, reads, writes, eng="dve"):
            if op1 is None:
                S.op(eng, (lambda e: e.tensor_scalar(out, a, s1, None, op0=op0)), reads=reads, writes=writes)
            else:
                S.op(eng, (lambda e: e.tensor_scalar(out, a, s1, s2, op0=op0, op1=op1)), reads=reads, writes=writes)

        def stt(out, a, sc, b, op0, op1, reads, writes, accum=None, eng="dve"):
            if accum is None:
                S.op(eng, (lambda e: e.scalar_tensor_tensor(out, a, sc, b, op0=op0, op1=op1)), reads=reads, writes=writes)
            else:
                S.op(eng, (lambda e: e.scalar_tensor_tensor(out, a, sc, b, op0=op0, op1=op1, accum_out=accum)),
                     reads=reads, writes=writes)

        def cp(eng, out, in_, reads, writes):
            if eng == "act":
                S.op("act", (lambda e: e.copy(out, in_)), reads=reads, writes=writes)
            else:
                S.op(eng, (lambda e: e.tensor_copy(out, in_)), reads=reads, writes=writes)

        def transpose(out, in_, ident, reads, writes, inc=True):
            S.op("pe", (lambda e: e.transpose(out, in_, ident)), reads=reads, writes=writes, inc=inc)

        def psb(i):
            return ps_t[i][:].bitcast(BF16)

        S.dma("sp", "c0", [(maskT[:], maskT_d), (dq[:], dq_d), (dk[:], dk_d), (invc[:], invc_d),
                           (identf[:], identf_d), (identb[:], identb_d), (onesd[:], onesd_d),
                           (poolm[:], poolm_d), (tril[:], tril_d)], writes=[CONST])
        S.dma("sp", "c1", [(badaT[:], bada_d), (n1T[:], n1_d), (n2T[:], n2_d), (fnT[:], fn_d),
                           (bsT[:], bs_d), (bpT[:], bp_d), (pscT[:], psc_d)], writes=[SMALL])
        S.dma("sp", "c2", [(vg[:, l, :], vg_d[l]) for l in range(2)], writes=[VG])
        S.op("dve", lambda e: e.memset(eps_t[:], EPS), writes=[CONST])
        tt(bps[:], bpT[:], pscT[:], ALU.mult, [SMALL], [SMALL])
        for l in range(2):
            for h in range(4):
                S.op("dve", lambda e, l=l, h=h: e.memset(Sst[:, l, h, :], 0.0), writes=[SB_[l][h]])
                S.op("dve", lambda e, l=l, h=h: e.memset(Sbf[:, l, h, :], 0.0), writes=[SBF[l][h]])

        tv, tb = temp_phase([("cc", [128, 8], F32), ("posi", [128, 16], I32), ("posf", [128, 16], F32),
                             ("invf", [128, 64], F32), ("ang", [128, 16, 64], F32), ("y", [128, 16, 64], F32),
                             ("ki", [128, 16, 64], I32), ("kf", [128, 16, 64], F32), ("m", [128, 16, 64], F32),
                             ("wsf", [128, 4, 128], F32), ("wsb", [128, 4, 128], BF16)])
        S.dma("sp", "c3", [(tv["cc"], c_d), (tv["posi"], pos_d), (tv["invf"], invf_d)],
              writes=[tb["cc"], tb["posi"], tb["invf"]])
        act(cact[:], tv["cc"], AF.Silu, [tb["cc"]], [CACT])

        cp("dve", tv["posf"], tv["posi"], [tb["posi"]], [tb["posf"]])
        tt(tv["ang"], tv["posf"].unsqueeze(2).broadcast_to([128, 16, 64]),
           tv["invf"].unsqueeze(1).broadcast_to([128, 16, 64]), ALU.mult, [tb["posf"], tb["invf"]], [tb["ang"]])
        TWO_PI = 2.0 * np.pi
        C1 = 6.28125
        C2 = float(TWO_PI - C1)
        for which, dst in (("sin", sin_t), ("cos", cos_t)):
            shift = 0.0 if which == "sin" else float(np.pi / 2)
            tsc(tv["y"], tv["ang"], shift, None, ALU.add, None, [tb["ang"]], [tb["y"]])
            tsc(tv["ki"], tv["y"], float(1.0 / TWO_PI), None, ALU.mult, None, [tb["y"]], [tb["ki"]])
            cp("dve", tv["kf"], tv["ki"], [tb["ki"]], [tb["kf"]])
            stt(tv["y"], tv["kf"], -C1, tv["y"], ALU.mult, ALU.add, [tb["kf"], tb["y"]], [tb["y"]])
            stt(tv["y"], tv["kf"], -C2, tv["y"], ALU.mult, ALU.add, [tb["kf"], tb["y"]], [tb["y"]])
            tsc(tv["m"], tv["y"], float(np.pi), float(-TWO_PI), ALU.is_gt, ALU.mult, [tb["y"]], [tb["m"]])
            tt(tv["y"], tv["y"], tv["m"], ALU.add, [tb["y"], tb["m"]], [tb["y"]])
            tsc(tv["m"], tv["y"], float(-np.pi), float(TWO_PI), ALU.is_lt, ALU.mult, [tb["y"]], [tb["m"]])
            tt(tv["y"], tv["y"], tv["m"], ALU.add, [tb["y"], tb["m"]], [tb["y"]])
            tsc(tv["y"], tv["y"], float(np.pi), float(-np.pi), ALU.min, ALU.max, [tb["y"]], [tb["y"]])
            act(dst[:], tv["y"], AF.Sin, [tb["y"]], [CS])


        for l in range(2):
            S.dma("sp", "c4", [(tv["wsf"], ws_d[l].rearrange("g t s -> t g s"))], writes=[tb["wsf"]])
            tt(tv["wsb"], tv["wsf"], tril[:].unsqueeze(1).broadcast_to([128, 4, 128]), ALU.mult,
               [tb["wsf"], CONST], [tb["wsb"]])
            for g in range(4):
                transpose(psb(7)[:, g * 128:(g + 1) * 128], tv["wsb"][:, g, :], identb[:],
                          [tb["wsb"], CONST], [PB[7]], inc=(g == 3))
            cp("dve", WsT[:, l, :, :], psb(7)[:, 0:512].rearrange("p (g t) -> p g t", g=4), [PB[7]], [WST])
        for l in range(2):
            S.op("dve", lambda e, l=l: e.memset(halo[:, l, :], 0.0), writes=[HALO[l]])

        def mod_block(l, blk):
            wv, wb = wload([(0, 512, wcols(wada_d[l], blk * 512, 512))], 8, 512)
            for j in range(4):
                col = blk * 4 + j
                mm_group(ps_t[6][:, col:col + 1],
                         [(wv[:, kd, j * 128:(j + 1) * 128], cact[:, kd:kd + 1]) for kd in range(8)],
                         [wb, CACT], [PB[6]])

        def mod_finish(l):
            tt(modT[:, l, :], ps_t[6][:, 0:48], badaT[:, l, :], ALU.add, [PB[6], SMALL], [MOD[l]])
            stt(g12[:, l, 0, :], modT[:, l, 8:16], 1.0, n1T[:, l, :], ALU.add, ALU.mult, [MOD[l], SMALL], [MOD[l]])
            stt(g12[:, l, 1, :], modT[:, l, 32:40], 1.0, n2T[:, l, :], ALU.add, ALU.mult, [MOD[l], SMALL], [MOD[l]])

        def compute_mod(l):
            for blk in range(12):
                mod_block(l, blk)
            mod_finish(l)

        def norm_to_h(gcol, shcol, modbuf):
            tv, tb = temp_phase([("sq", [128, 8, 512], BF16), ("rt", [128, 512], F32),
                                 ("rstd", [128, 512], F32), ("t0", [128, 512], F32), ("t1", [128, 512], F32)])
            for half in range(2):
                hs = slice(half * 512, (half + 1) * 512)
                for kd in range(8):
                    tt(tv["sq"][:, kd, :], xT[:, kd, hs], xT[:, kd, hs], ALU.mult, [XB[kd][half]], [tb["sq"]])
                mm_group(ps_t[5][:], [(onesd[:], tv["sq"][:, kd, :]) for kd in range(8)], [tb["sq"], CONST], [PB[5]])
                act(tv["rt"], ps_t[5][:], AF.Sqrt, [PB[5], CONST], [tb["rt"]], bias=eps_t[:, 0:1], scale=1.0)
                S.op("dve", lambda e: e.reciprocal(tv["rstd"], tv["rt"]), reads=[tb["rt"]], writes=[tb["rstd"]])
                for kd in range(8):
                    t = "t0" if kd % 2 == 0 else "t1"
                    tt(tv[t], xT[:, kd, hs], tv["rstd"], ALU.mult, [XB[kd][half], tb["rstd"]], [tb[t]])
                    act(hT[:, kd, hs], tv[t], AF.Identity, [tb[t], modbuf], [HB[kd][half]],
                        bias=shcol(kd), scale=gcol(kd))

        def branch(l, n, first):
            tv, tb = temp_phase([("sg0", [128, 512], F32), ("sg1", [128, 512], F32),
                                 ("pr0", [128, 512], F32), ("pr1", [128, 512], F32)])
            it = 0
            for mb in range(2):
                wa, wab = wload([(0, 512, wcols(wbr_d[l, n], mb * 512, 512))], 8, 512)
                wg, wgb = wload([(0, 512, wcols(win_d[l], OFF_GATE + n * 1024 + mb * 512, 512))], 8, 512)
                for mi in range(4):
                    m = mb * 4 + mi
                    for half in range(2):
                        hs = slice(half * 512, (half + 1) * 512)
                        bpb, gtb = (0, 1) if it % 2 == 0 else (2, 3)
                        mm_group(ps_t[bpb][:], [(wa[:, kc, mi * 128:(mi + 1) * 128], ynT[:, kc, hs]) for kc in range(8)],
                                 [wab] + [YB(kc) for kc in range(8)], [PB[bpb]])
                        mm_group(ps_t[gtb][:], [(wg[:, kd, mi * 128:(mi + 1) * 128], hT[:, kd, hs]) for kd in range(8)],
                                 [wgb] + [HB[kd][half] for kd in range(8)], [PB[gtb]])
                        sg = "sg0" if it % 2 == 0 else "sg1"
                        pr = "pr0" if it % 2 == 0 else "pr1"
                        act(tv[sg], ps_t[gtb][:], AF.Sigmoid, [PB[gtb]], [tb[sg]])
                        if first:
                            tt(merged[:, m, hs], ps_t[bpb][:], tv[sg], ALU.mult, [PB[bpb], tb[sg]], [MB(m, half)])
                        else:
                            tt(tv[pr], ps_t[bpb][:], tv[sg], ALU.mult, [PB[bpb], tb[sg]], [tb[pr]])
                            tt(merged[:, m, hs], merged[:, m, hs], tv[pr], ALU.add, [tb[pr], MB(m, half)], [MB(m, half)])
                        it += 1

        def sgu(l):
            tv, tb = temp_phase([("ss", [128, 16], F32), ("ss8", [128, 8], F32), ("rt8", [128, 8], F32),
                                 ("rstd8", [128, 8], F32), ("junk", [128, 512], BF16),
                                 ("u0", [128, 512], BF16), ("u1", [128, 512], BF16),
                                 ("gn0", [128, 512], BF16), ("gn1", [128, 512], BF16),
                                 ("y0", [128, 512], BF16), ("y1", [128, 512], BF16)]
                                + [(f"gv{c}", [128, 1024], BF16) for c in range(NCH)])
            S.op("dve", lambda e: e.memset(tv["ss"], 0.0), writes=[tb["ss"]])
            for blk in range(2):
                wv, wb = wload([(0, 512, wcols(win_d[l], OFF_VS + blk * 512, 512))], 8, 512)
                for c in range(NCH):
                    b = c % 2
                    cs = slice(c * 128, (c + 1) * 128)
                    mm_group(ps_t[b][:], [(hT[:, kd, cs], wv[:, kd, :]) for kd in range(8)],
                             [wb] + [HB[kd][c // 4] for kd in range(8)], [PB[b]])
                    gvc = tv[f"gv{c}"][:, blk * 512:(blk + 1) * 512]
                    act(gvc, ps_t[b][:], AF.Gelu_apprx_tanh, [PB[b]], [tb[f"gv{c}"]])
                    stt(tv["junk"], gvc, 1.0, gvc,
                        ALU.mult, ALU.mult, [tb[f"gv{c}"]], [tb["junk"], tb["ss"]], accum=tv["ss"][:, c * 2 + blk:c * 2 + blk + 1])
            ssv = tv["ss"].rearrange("p (c b) -> p c b", b=2)
            tt(tv["ss8"], ssv[:, :, 0], ssv[:, :, 1], ALU.add, [tb["ss"]], [tb["ss8"]])
            act(tv["rt8"], tv["ss8"], AF.Sqrt, [tb["ss8"], CONST], [tb["rt8"]], bias=eps_t[:, 0:1], scale=1.0 / 1024.0)
            S.op("dve", lambda e: e.reciprocal(tv["rstd8"], tv["rt8"]), reads=[tb["rt8"]], writes=[tb["rstd8"]])
            for blk in range(2):
                wv, wb = wload([(0, 512, wcols(win_d[l], OFF_U + blk * 512, 512))], 8, 512)
                fs = slice(blk * 512, (blk + 1) * 512)

                def sA(c, blk=blk, wv=wv, wb=wb, fs=fs):
                    b = c % 2
                    cs = slice(c * 128, (c + 1) * 128)
                    u, gn = f"u{b}", f"gn{b}"
                    mm_group(ps_t[b][:], [(hT[:, kd, cs], wv[:, kd, :]) for kd in range(8)],
                             [wb] + [HB[kd][c // 4] for kd in range(8)], [PB[b]])
                    act(tv[u], ps_t[b][:], AF.Gelu_apprx_tanh, [PB[b]], [tb[u]])
                    stt(tv[gn], tv[f"gv{c}"][:, fs], tv["rstd8"][:, c:c + 1], vg[:, l, fs], ALU.mult, ALU.mult,
                        [tb[f"gv{c}"], tb["rstd8"], VG], [tb[gn]])

                def sB(c, blk=blk):
                    b = c % 2
                    u, gn, y = f"u{b}", f"gn{b}", f"y{b}"
                    for gg in range(2):
                        g = 2 * blk + gg
                        S.op("pe", (lambda e, b=b, gg=gg, g=g, gn=gn: e.matmul(
                            ps_t[2 + b][:, gg * 256:(gg + 1) * 256], lhsT=WsT[:, l, g, :],
                            rhs=tv[gn][:, gg * 256:(gg + 1) * 256], start=True, stop=True)),
                            reads=[WST, tb[gn]], writes=[PB[2 + b]], inc=(gg == 1))
                    for gg in range(2):
                        g = 2 * blk + gg
                        stt(tv[y][:, gg * 256:(gg + 1) * 256], ps_t[2 + b][:, gg * 256:(gg + 1) * 256],
                            bsT[:, l, g:g + 1], tv[u][:, gg * 256:(gg + 1) * 256], ALU.add, ALU.mult,
                            [PB[2 + b], SMALL, tb[u]], [tb[y]])

                def sC(c, blk=blk):
                    b = c % 2
                    cs = slice(c * 128, (c + 1) * 128)
                    y = f"y{b}"
                    for j in range(4):
                        transpose(psb(4 + b)[:, j * 128:(j + 1) * 128], tv[y][:, j * 128:(j + 1) * 128], identb[:],
                                  [tb[y], CONST], [PB[4 + b]], inc=(j == 3))
                    cp("act", ynT[:, 4 * blk:4 * blk + 4, cs],
                       psb(4 + b)[:, 0:512].rearrange("p (k t) -> p k t", k=4),
                       [PB[4 + b]], [YB(4 * blk + j) for j in range(4)])

                for step in range(NCH + 2):
                    if 0 <= step - 2 < NCH:
                        sC(step - 2)
                    if 0 <= step - 1 < NCH:
                        sB(step - 1)
                    if step < NCH:
                        sA(step)

        def retention(l, grp):
            tv, tb = temp_phase([("ss0", [128, 8], F32), ("ss1", [128, 8], F32), ("rt", [128, 8], F32), ("rstd", [128, 8], F32),
                                 ("junk", [128, 256], BF16),
                                 ("a0", [128, 4, 64], F32), ("a1", [128, 4, 64], F32),
                                 ("b10", [128, 2, 64], F32), ("b11", [128, 2, 64], F32),
                                 ("b20", [128, 2, 64], F32), ("b21", [128, 2, 64], F32),
                                 ("qk0", [128, 256], BF16), ("qk1", [128, 256], BF16),
                                 ("xq0", [128, 256], F32), ("xq1", [128, 256], F32),
                                 ("v0", [128, 256], BF16), ("v1", [128, 256], BF16), ("v2", [128, 256], BF16), ("v3", [128, 256], BF16),
                                 ("qkT0", [128, 3, 128], BF16), ("qkT1", [128, 3, 128], BF16),
                                 ("kd0", [128, 128], BF16), ("kd1", [128, 128], BF16), ("kd2", [128, 128], BF16),
                                 ("sT0", [128, 128], BF16), ("sT1", [128, 128], BF16)])
            o_v = [arena[:, p * 4096:(p + 1) * 4096].bitcast(F32).rearrange("p (c e) -> p c e", c=8) for p in range(2)]
            g_v = [arena[:, 8192 + p * 2048:8192 + (p + 1) * 2048].rearrange("p (c e) -> p c e", c=8) for p in range(2)]
            ytm_v = arena[:, 12288:14336].rearrange("p (c e) -> p c e", c=8)
            OB = [Buf("ret_o0"), Buf("ret_o1")]
            GB = [Buf("ret_g0"), Buf("ret_g1")]
            YTB = Buf("ret_ytm")
            S.inherit(OB + GB + [YTB], AR[0:16])
            wts = {}

            def names_it(it):
                b = it % 2
                return (b, f"a{b}", f"b1{b}", f"b2{b}", f"qk{b}", f"v{it % 4}", f"qkT{b}", f"kd{it % 3}", f"sT{b}", f"xq{b}")

            cur_h = [0]

            def names(c):
                return names_it(cur_h[0] * NCH + c)

            def stA1(h, c):
                if c == 0:
                    wa, wab = wload([(0, 512, wcols(win_d[l], 512 * h, 512))], 8, 512)
                    wg, wgb = wload([(0, 256, wcols(win_d[l], OFF_G + 256 * h, 256))], 8, 256)
                    wts[h] = (wa, wab, wg, wgb)
                    S.op("dve", lambda e, h=h: e.memset(tv[f"ss{h % 2}"], 0.0), writes=[tb[f"ss{h % 2}"]])
                wa, wab, wg, wgb = wts[h]
                b, a, b1, b2, qk, v, qkT, kdt, sT, xq = names(c)
                cs = slice(c * 128, (c + 1) * 128)
                hb = [HB[kd][c // 4] for kd in range(8)]
                QKV, G = ps_t[b], ps_t[2 + b]
                mm_group(QKV[:], [(hT[:, kd, cs], wa[:, kd, :]) for kd in range(8)], [wab] + hb, [PB[b]])
                mm_group(G[:, 0:256], [(hT[:, kd, cs], wg[:, kd, :]) for kd in range(8)], [wgb] + hb, [PB[2 + b]])
                cp("act", tv[xq], QKV[:, 0:256], [PB[b]], [tb[xq]])
                cp("act", tv[v], QKV[:, 256:512], [PB[b]], [tb[v]])
                act(g_v[h % 2][:, c, :], G[:, 0:256], AF.Silu, [PB[2 + b]], [GB[h % 2]])

            def stA2(h, c):
                b, a, b1, b2, qk, v, qkT, kdt, sT, xq = names(c)
                n = grp * NCH + c
                q4 = tv[xq].rearrange("p (a f) -> p a f", a=4)
                q22 = tv[xq].rearrange("p (a b f) -> p a b f", a=2, b=2)
                cosb = cos_t[:, n, :].unsqueeze(1).broadcast_to([128, 4, 64])
                sinb = sin_t[:, n, :].unsqueeze(1).broadcast_to([128, 2, 64])
                tt(tv[a], q4, cosb, ALU.mult, [tb[xq], CS], [tb[a]])
                tt(tv[b1], q22[:, :, 1, :], sinb, ALU.mult, [tb[xq], CS], [tb[b1]])
                tt(tv[b2], q22[:, :, 0, :], sinb, ALU.mult, [tb[xq], CS], [tb[b2]])
                a22 = tv[a].rearrange("p (a b) f -> p a b f", b=2)
                qk22 = tv[qk].rearrange("p (a b f) -> p a b f", a=2, b=2)
                stt(qk22[:, :, 0, :], tv[b1], -1.0, a22[:, :, 0, :], ALU.mult, ALU.add, [tb[a], tb[b1]], [tb[qk]])
                tt(qk22[:, :, 1, :], a22[:, :, 1, :], tv[b2], ALU.add, [tb[a], tb[b2]], [tb[qk]])
                tt(tv[kdt], tv[qk][:, 128:256], dk[:, h:h + 1].broadcast_to([128, 128]), ALU.mult,
                   [tb[qk], CONST], [tb[kdt]])

            def stB1(h, c):
                b, a, b1, b2, qk, v, qkT, kdt, sT, xq = names(c)
                transpose(psb(4)[:, 0:128], tv[qk][:, 0:128], identb[:], [tb[qk], CONST], [PB[4]], inc=False)
                transpose(psb(4)[:, 128:256], tv[qk][:, 128:256], identb[:], [tb[qk], CONST], [PB[4]])
                cp("act", tv[qkT][:, 0, :], psb(4)[:, 0:128], [PB[4]], [tb[qkT]])
                cp("act", tv[qkT][:, 2, :], psb(4)[:, 128:256], [PB[4]], [tb[qkT]])
                tt(tv[qkT][:, 1, :], tv[qkT][:, 0, :], dq[:, h, :], ALU.mult, [tb[qkT], CONST], [tb[qkT]])

            def stB2(h, c):
                b, a, b1, b2, qk, v, qkT, kdt, sT, xq = names(c)
                S.op("pe", (lambda e, qkT=qkT: e.matmul(ps_t[5][:, 0:128], lhsT=tv[qkT][:, 2, :], rhs=tv[qkT][:, 0, :],
                                                        start=True, stop=True)), reads=[tb[qkT]], writes=[PB[5]])
                tt(tv[sT], ps_t[5][:, 0:128], maskT[:, h, :], ALU.mult, [PB[5], CONST], [tb[sT]])

            def tail(h):
                hp = h % 2
                ss = f"ss{hp}"
                act(tv["rt"], tv[ss], AF.Sqrt, [tb[ss], CONST], [tb["rt"]], bias=eps_t[:, 0:1], scale=1.0 / 256.0)
                S.op("dve", lambda e: e.reciprocal(tv["rstd"], tv["rt"]), reads=[tb["rt"]], writes=[tb["rstd"]])
                for c in range(NCH):
                    stt(ytm_v[:, c, :], o_v[hp][:, c, :], tv["rstd"][:, c:c + 1], g_v[hp][:, c, :], ALU.mult, ALU.mult,
                        [OB[hp], tb["rstd"], GB[hp]], [YTB])
                for c4 in range(2):
                    pbi = 6 + c4
                    for cc in range(4):
                        for e2 in range(2):
                            transpose(psb(pbi)[:, (cc * 2 + e2) * 128:(cc * 2 + e2 + 1) * 128],
                                      ytm_v[:, c4 * 4 + cc, e2 * 128:(e2 + 1) * 128], identb[:],
                                      [YTB, CONST], [PB[pbi]], inc=(cc == 3 and e2 == 1))
                    cp("act", ynT[:, 2 * h:2 * h + 2, c4 * 512:(c4 + 1) * 512].rearrange("p e (c t) -> p e c t", c=4),
                       psb(pbi)[:, 0:1024].rearrange("p (c e t) -> p e c t", c=4, e=2),
                       [PB[pbi]], [YB(2 * h), YB(2 * h + 1)])

            def stC(h, c):
                b, a, b1, b2, qk, v, qkT, kdt, sT, xq = names(c)
                hp = h % 2
                S.op("pe", (lambda e, sT=sT, v=v: e.matmul(ps_t[7][:, 0:256], lhsT=tv[sT], rhs=tv[v], start=True, stop=False)),
                     reads=[tb[sT], tb[v]], writes=[PB[7]], inc=False)
                S.op("pe", (lambda e, qkT=qkT, h=h: e.matmul(ps_t[7][:, 0:256], lhsT=tv[qkT][:, 1, :], rhs=Sbf[:, l, h, :],
                                                             start=False, stop=True)),
                     reads=[tb[qkT], SBF[l][h]], writes=[PB[7]])
                S.op("pe", (lambda e, kdt=kdt, v=v: e.matmul(ps_t[6][:, 0:256], lhsT=tv[kdt], rhs=tv[v], start=True, stop=True)),
                     reads=[tb[kdt], tb[v]], writes=[PB[6]])
                tsc(Sst[:, l, h, :], Sst[:, l, h, :], K["dchunk"][h], None, ALU.mult, None, [SB_[l][h]], [SB_[l][h]])
                tt(Sst[:, l, h, :], ps_t[6][:, 0:256], Sst[:, l, h, :], ALU.add, [PB[6], SB_[l][h]], [SB_[l][h]])
                cp("act", Sbf[:, l, h, :], Sst[:, l, h, :], [SB_[l][h]], [SBF[l][h]])
                cp("act", o_v[hp][:, c, :], ps_t[7][:, 0:256], [PB[7]], [OB[hp]])
                act(tv["junk"], o_v[hp][:, c, :], AF.Square, [OB[hp]], [tb["junk"], tb[f"ss{hp}"]],
                    accum=tv[f"ss{hp}"][:, c:c + 1])
                if c == NCH - 1:
                    tail(h)

            stages = [stA1, stA2, stB1, stB2, stC]
            NIT = 4 * NCH
            for step in range(NIT + len(stages) - 1):
                for si in range(len(stages) - 1, -1, -1):
                    it = step - si
                    if 0 <= it < NIT:
                        cur_h[0] = it // NCH
                        stages[si](it // NCH, it % NCH)
            toks = []
            for bb in OB + GB + [YTB]:
                if bb.writer is not None:
                    toks.append(bb.writer)
                toks.extend(bb.readers)
            for bb in AR[0:16]:
                bb.readers.extend(toks)

        def pool(l, grp):
            tv, tb = temp_phase([("p0", [128, 1024], BF16), ("p1", [128, 1024], BF16),
                                 ("pl0", [128, 8, 128], BF16), ("pl1", [128, 8, 128], BF16),
                                 ("pf", [128, 2, 128], F32)])
            S.dma("pool", "wp", [(wp[:], wpool_d[l].rearrange("g (cc p) e -> p g cc e", p=128))], writes=[WP])
            w0, w0b = wload([(0, 512, wcols(win_d[l], OFF_P, 512))], 8, 512)
            w1, w1b = wload([(0, 512, wcols(win_d[l], OFF_P + 512, 512))], 8, 512)
            def pA(c):
                b = c % 2
                cs = slice(c * 128, (c + 1) * 128)
                hb = [HB[kd][c // 4] for kd in range(8)]
                cur, curb = tv[f"p{b}"], tb[f"p{b}"]
                for blk, (wv, wb) in enumerate(((w0, w0b), (w1, w1b))):
                    mm_group(ps_t[blk][:], [(hT[:, kd, cs], wv[:, kd, :]) for kd in range(8)], [wb] + hb, [PB[blk]])
                    cp("act" if blk == 0 else "dve", cur[:, blk * 512:(blk + 1) * 512], ps_t[blk][:], [PB[blk]], [curb])
                if grp == 0 and c == NCH - 1:
                    cp("act", halo[:, l, :], cur, [curb], [HALO[l]])

            def pB(c):
                b = c % 2
                cur, curb = tv[f"p{b}"], tb[f"p{b}"]
                if c == 0:
                    prev, prevb = halo[:, l, :], HALO[l]
                else:
                    prev, prevb = tv[f"p{1 - b}"], tb[f"p{1 - b}"]
                first_chunk = (grp == 0 and c == 0)
                for kc in range(8):
                    gi = kc // 2
                    pbi = 2 + kc // 4
                    dst = ps_t[pbi][:, (kc % 4) * 128:(kc % 4 + 1) * 128]
                    last = (kc % 4 == 3)
                    if first_chunk:
                        S.op("pe", (lambda e, dst=dst, kc=kc, gi=gi, cur=cur: e.matmul(
                            dst, lhsT=cur[:, kc * 128:(kc + 1) * 128], rhs=poolm[:, 2, gi, :], start=True, stop=True)),
                            reads=[curb, CONST], writes=[PB[pbi]], inc=last)
                    else:
                        S.op("pe", (lambda e, dst=dst, kc=kc, gi=gi, cur=cur: e.matmul(
                            dst, lhsT=cur[:, kc * 128:(kc + 1) * 128], rhs=poolm[:, 0, gi, :], start=True, stop=False)),
                            reads=[curb, CONST], writes=[PB[pbi]], inc=False)
                        S.op("pe", (lambda e, dst=dst, kc=kc, gi=gi, prev=prev: e.matmul(
                            dst, lhsT=prev[:, kc * 128:(kc + 1) * 128], rhs=poolm[:, 1, gi, :], start=False, stop=True)),
                            reads=[prevb, CONST], writes=[PB[pbi]], inc=last)
                pl, plb = tv[f"pl{b}"], tb[f"pl{b}"]
                for gi in range(4):
                    pbi = 2 + gi // 2
                    src = ps_t[pbi][:, (gi % 2) * 256:(gi % 2 + 1) * 256].rearrange("p (k t) -> p k t", k=2)
                    if first_chunk:
                        cp("act", tv["pf"], src, [PB[pbi]], [tb["pf"]])
                        tt(pl[:, 2 * gi:2 * gi + 2, :], tv["pf"], invc[:, gi, :].unsqueeze(1).broadcast_to([128, 2, 128]),
                           ALU.mult, [tb["pf"], CONST], [plb])
                    else:
                        act(pl[:, 2 * gi:2 * gi + 2, :], src, AF.Copy, [PB[pbi]], [plb], scale=1.0 / POOL_W[gi])

            def pC(c):
                b = c % 2
                cs = slice(c * 128, (c + 1) * 128)
                pl, plb = tv[f"pl{b}"], tb[f"pl{b}"]
                for ko in range(8):
                    gi, ec = ko // 2, ko % 2
                    pbi = 4 + ko // 4
                    dst = ps_t[pbi][:, (ko % 4) * 128:(ko % 4 + 1) * 128]
                    for cc in range(2):
                        S.op("pe", (lambda e, dst=dst, gi=gi, ec=ec, cc=cc, pl=pl: e.matmul(
                            dst, lhsT=wp[:, gi, cc, ec * 128:(ec + 1) * 128], rhs=pl[:, 2 * gi + cc, :],
                            start=(cc == 0), stop=(cc == 1))),
                            reads=[WP, plb], writes=[PB[pbi]], inc=(cc == 1 and ko % 4 == 3))
                for hb4 in range(2):
                    pbi = 4 + hb4
                    src = ps_t[pbi][:].rearrange("p (k t) -> p k t", k=4)
                    for j in range(4):
                        ko = hb4 * 4 + j
                        act(ynT[:, ko, cs], src[:, j, :], AF.Identity, [PB[pbi], SMALL], [YB(ko)],
                            bias=bps[:, l, ko:ko + 1], scale=pscT[:, l, ko:ko + 1])

            for step in range(NCH + 2):
                if 0 <= step - 2 < NCH:
                    pC(step - 2)
                if 0 <= step - 1 < NCH:
                    pB(step - 1)
                if step < NCH:
                    pA(step)

        def out_proj(l):
            mbf = ynT
            for kd in range(8):
                cp("act" if kd % 2 == 0 else "dve", mbf[:, kd, :], merged[:, kd, :], [MB(kd, 0), MB(kd, 1)], [YB(kd)])
            it = 0
            for mb in range(2):
                wv, wb = wload([(0, 512, wcols(wout_d[l], mb * 512, 512))], 8, 512)
                for mi in range(4):
                    m = mb * 4 + mi
                    for half in range(2):
                        hs = slice(half * 512, (half + 1) * 512)
                        pbi = it % 2
                        mm_group(ps_t[pbi][:], [(wv[:, kd, mi * 128:(mi + 1) * 128], mbf[:, kd, hs]) for kd in range(8)],
                                 [wb] + [YB(kd) for kd in range(8)], [PB[pbi]])
                        stt(xT[:, m, hs], ps_t[pbi][:], modT[:, l, 16 + m:17 + m], xT[:, m, hs], ALU.mult, ALU.add,
                            [PB[pbi], MOD[l], XB[m][half]], [XB[m][half]])
                        it += 1

        def ffn(l, grp=1):
            tv, tb = temp_phase([("r0", [128, 512], F32), ("r1", [128, 512], F32)])
            it = 0
            for q in range(4):
                hq = hid[q % 2]
                hqb = AR[8 * (q % 2):8 * (q % 2) + 8]
                for j in range(2):
                    wv, wb = wload([(0, 512, wcols(wff1_d[l], q * 1024 + j * 512, 512))], 8, 512)
                    for mi in range(4):
                        fc = j * 4 + mi
                        for half in range(2):
                            hs = slice(half * 512, (half + 1) * 512)
                            pbi = it % 2
                            r = f"r{it % 2}"
                            mm_group(ps_t[pbi][:], [(wv[:, kd, mi * 128:(mi + 1) * 128], hT[:, kd, hs]) for kd in range(8)],
                                     [wb] + [HB[kd][half] for kd in range(8)], [PB[pbi]])
                            act(tv[r], ps_t[pbi][:], AF.Relu, [PB[pbi]], [tb[r]])
                            tt(hq[:, fc, hs], tv[r], tv[r], ALU.mult, [tb[r]], [hqb[fc]])
                            it += 1
                for mb in range(2):
                    wv, wb = wload([(0, 512, wff2_d[l][q * 1024:(q + 1) * 1024, mb * 512:(mb + 1) * 512]
                                     .rearrange("(k p) w -> p k w", p=128))], 8, 512)
                    for mi in range(4):
                        m = mb * 4 + mi
                        for half in range(2):
                            hs = slice(half * 512, (half + 1) * 512)
                            pbi = 2 + it % 2
                            mm_group(ps_t[pbi][:], [(wv[:, fc, mi * 128:(mi + 1) * 128], hq[:, fc, hs]) for fc in range(8)],
                                     [wb] + list(hqb), [PB[pbi]])
                            stt(xT[:, m, hs], ps_t[pbi][:], modT[:, l, 40 + m:41 + m], xT[:, m, hs], ALU.mult, ALU.add,
                                [PB[pbi], MOD[l], XB[m][half]], [XB[m][half]])
                            it += 1
                if grp == 0 and l == 0:
                    for blk in range(3 * q, 3 * q + 3):
                        mod_block(1, blk)
                    if q == 3:
                        mod_finish(1)

        def load_x(grp):
            tv, tb = temp_phase([("xs0", [128, 1024], F32), ("xs1", [128, 1024], F32)])
            for c in range(NCH):
                b = c % 2
                xs, xsb = tv[f"xs{b}"], tb[f"xs{b}"]
                r0 = grp * TG + c * 128
                S.dma("sp", f"xin{b}", [(xs, x_d[r0:r0 + 128, :])], writes=[xsb])
                for q in range(2):
                    pbi = 2 * b + q
                    for j in range(4):
                        kd = q * 4 + j
                        transpose(ps_t[pbi][:, j * 128:(j + 1) * 128], xs[:, kd * 128:(kd + 1) * 128], identf[:],
                                  [xsb, CONST], [PB[pbi]], inc=(j == 3))
                    cp("act" if q == 0 else "dve", xT[:, q * 4:q * 4 + 4, c * 128:(c + 1) * 128],
                       ps_t[pbi][:].rearrange("p (k t) -> p k t", k=4), [PB[pbi]],
                       [XB[q * 4 + j][c // 4] for j in range(4)])

        def final_store(grp):
            tv, tb = temp_phase([("sq", [128, 8, 512], BF16), ("rt", [128, 512], F32),
                                 ("rstd", [128, 1024], F32), ("os0", [128, 1024], F32), ("os1", [128, 1024], F32)])
            on = merged
            for half in range(2):
                hs = slice(half * 512, (half + 1) * 512)
                for kd in range(8):
                    act(tv["sq"][:, kd, :], xT[:, kd, hs], AF.Square, [XB[kd][half]], [tb["sq"]])
                mm_group(ps_t[5][:], [(onesd[:], tv["sq"][:, kd, :]) for kd in range(8)], [tb["sq"], CONST], [PB[5]])
                act(tv["rt"], ps_t[5][:], AF.Sqrt, [PB[5], CONST], [tb["rt"]], bias=eps_t[:, 0:1], scale=1.0)
                S.op("dve", lambda e, hs=hs: e.reciprocal(tv["rstd"][:, hs], tv["rt"]), reads=[tb["rt"]], writes=[tb["rstd"]])
                for kd in range(8):
                    stt(on[:, kd, hs], xT[:, kd, hs], fnT[:, kd:kd + 1], tv["rstd"][:, hs], ALU.mult, ALU.mult,
                        [XB[kd][half], SMALL, tb["rstd"]], [MB(kd, half)])
            for c in range(NCH):
                b = c % 2
                os_, osb = tv[f"os{b}"], tb[f"os{b}"]
                for q in range(2):
                    pbi = 2 * b + q
                    for j in range(4):
                        kd = q * 4 + j
                        transpose(ps_t[pbi][:, j * 128:(j + 1) * 128], on[:, kd, c * 128:(c + 1) * 128], identf[:],
                                  [MB(kd, c // 4), CONST], [PB[pbi]], inc=(j == 3))
                    cp("act" if q == 0 else "dve", os_[:, q * 512:(q + 1) * 512], ps_t[pbi][:], [PB[pbi]], [osb])
                r0 = grp * TG + c * 128
                S.dma("sp", f"xout{b}", [(out_d[r0:r0 + 128, :], os_)], reads=[osb])
            return [tb["os0"], tb["os1"]]

        import os
        stage = os.environ.get("K_STAGE", "")
        order = ["load", "mod", "norm", "sgu", "br1", "ret", "br0", "pool", "br2", "out", "norm2", "ffn"]
        lim = order.index(stage) if stage in order else 99
        ngrp = 1 if stage else NG
        nlay = 1 if stage else 2
        if lim >= 1:
            compute_mod(0)
        last_out = []
        for grp in range(ngrp):
            load_x(grp)
            for l in range(nlay):
                if lim >= 2:
                    norm_to_h(lambda kd, l=l: g12[:, l, 0, kd:kd + 1], lambda kd, l=l: modT[:, l, kd:kd + 1], MOD[l])
                if lim >= 3:
                    retention(l, grp)
                if lim >= 4:
                    branch(l, 0, True)
                if lim >= 5:
                    sgu(l)
                if lim >= 6:
                    branch(l, 1, False)
                if lim >= 7:
                    pool(l, grp)
                if lim >= 8:
                    branch(l, 2, False)
                if lim >= 9:
                    out_proj(l)
                if lim >= 10:
                    norm_to_h(lambda kd, l=l: g12[:, l, 1, kd:kd + 1], lambda kd, l=l: modT[:, l, 24 + kd:25 + kd], MOD[l])
                if lim >= 11:
                    ffn(l, grp)
            last_out = final_store(grp)
        S.final_wait("sp", last_out)
        S.emit()
    return nc


_CACHE = {}


def _perm_w_in(w):
    idx = []
    for h in range(4):
        idx += list(range(OFF_Q + 128 * h, OFF_Q + 128 * h + 128))
        idx += list(range(OFF_K + 128 * h, OFF_K + 128 * h + 128))
        idx += list(range(OFF_V + 256 * h, OFF_V + 256 * h + 256))
    idx += list(range(2048, DIN))
    return np.ascontiguousarray(w[:, :, np.asarray(idx)])


def _prep_inputs(x, c, positions, w_ada, b_ada, norm1, norm2, w_in, ws_gmlp, bs_gmlp, vnorm_gmlp,
                 w_pool, b_pool, pool_scale, w_branch, w_out, w_ff1, w_ff2, final_norm):
    f = lambda a: np.ascontiguousarray(np.asarray(a, dtype=np.float32))
    K = _consts()

    def colT(v, n):
        v = f(v)
        lead = v.shape[:-1]
        return np.ascontiguousarray(np.moveaxis(v.reshape(lead + (n, 128)), -1, 0))

    shared = {
        "w_ada": f(w_ada), "b_adaT": colT(b_ada, 48), "norm1T": colT(norm1, 8), "norm2T": colT(norm2, 8),
        "fnT": colT(final_norm, 8), "w_in": _perm_w_in(f(w_in)), "ws_gmlp": f(ws_gmlp),
        "bsT": np.ascontiguousarray(np.transpose(f(bs_gmlp), (2, 0, 1))),
        "vg_bc": np.ascontiguousarray(np.broadcast_to(f(vnorm_gmlp)[:, None, :], (2, 128, 1024))),
        "w_pool": f(w_pool), "b_poolT": colT(f(b_pool).reshape(2, 1024), 8),
        "pscaleT": colT(pool_scale, 8), "w_branch": f(w_branch), "w_out": f(w_out),
        "w_ff1": f(w_ff1), "w_ff2": f(w_ff2),
        "maskT": K["maskT"], "dq": K["dq"], "dk": K["dk"], "invf": K["invf"], "tril": K["tril"],
        "identf": K["identf"], "identb": K["identb"], "onesd": K["onesd"], "poolm": K["poolm"], "invc": K["invc"],
    }
    x = f(x)
    c = f(c)
    pos = np.asarray(positions).astype(np.int32)
    in_maps = []
    for b in range(NB):
        m = dict(shared)
        m["x"] = np.ascontiguousarray(x[b])
        m["c_col"] = np.ascontiguousarray(c[b].reshape(8, 128).T)
        m["pos"] = np.ascontiguousarray(pos[b].reshape(16, 128).T)
        in_maps.append(m)
    return in_maps


def kernel(**inputs):
    if "nc" not in _CACHE:
        _CACHE["nc"] = build_program()
    nc = _CACHE["nc"]
    in_maps = _prep_inputs(**inputs)
    res = run_bass_kernel_spmd(nc, in_maps, core_ids=list(range(NB)))
    out = np.stack([np.asarray(r["out"], dtype=np.float32) for r in res.results], axis=0)
    return out
```

```python
import contextlib
import numpy as np
import ml_dtypes
import concourse.bass as bass
import concourse.mybir as mybir
from concourse.bass_utils import run_bass_kernel_spmd

F32 = mybir.dt.float32
BF16 = mybir.dt.bfloat16
I32 = mybir.dt.int32
AF = mybir.ActivationFunctionType
ALU = mybir.AluOpType

D = 1024
SEQ = 2048
NB = 8
TG = 1024
NG = SEQ // TG
NCH = TG // 128
DIN = 9216
DFF = 4096
EPS = 1e-6
OFF_Q, OFF_K, OFF_V, OFF_G, OFF_U, OFF_VS, OFF_P, OFF_GATE = 0, 512, 1024, 2048, 3072, 4096, 5120, 6144
POOL_W = (2, 4, 8, 16)
ENG = ("pe", "act", "dve", "pool", "sp")


class Buf:
    __slots__ = ("name", "writer", "readers", "excl")

    def __init__(self, name, excl=False):
        self.name = name
        self.writer = None
        self.readers = []
        self.excl = excl


class Sched:
    def __init__(self, nc, stack, same_engine_sync=True):
        self.nc = nc
        self.stack = stack
        self.sem = {e: stack.enter_context(nc.semaphore("prog_" + e)) for e in ENG}
        self.cnt = {e: 0 for e in ENG}
        self.seen = {e: {} for e in ENG}
        self.same = same_engine_sync
        self.dsem = {}
        self.prog = {e: [] for e in ENG}

    def _semh(self, kind, key):
        return self.sem[key] if kind == "e" else self.dsem[key][0]

    def _need(self, eng, deps, tok):
        if tok is None:
            return
        kind, key, count = tok
        if kind == "e" and key == eng:
            if not self.same or eng in ("pe", "sp", "pool"):
                return
        k = (kind, key)
        if deps.get(k, 0) < count:
            deps[k] = count

    def _waits(self, eng, deps):
        out = []
        for (kind, key), count in deps.items():
            if self.seen[eng].get((kind, key), 0) >= count:
                continue
            out.append((self._semh(kind, key), count))
            self.seen[eng][(kind, key)] = count
        return out

    def _deps(self, eng, reads, writes):
        deps = {}
        for b in reads:
            self._need(eng, deps, b.writer)
            if b.excl:
                for r in b.readers:
                    if not (r[0] == "e" and r[1] == eng):
                        self._need(eng, deps, r)
        for b in writes:
            self._need(eng, deps, b.writer)
            for r in b.readers:
                self._need(eng, deps, r)
        return deps

    def _mark(self, tok, reads, writes):
        for b in reads:
            b.readers.append(tok)
            if len(b.readers) > 64:
                best = {}
                for (k, key, c) in b.readers:
                    if best.get((k, key), 0) < c:
                        best[(k, key)] = c
                b.readers = [(k, key, c) for (k, key), c in best.items()]
        for b in writes:
            b.writer = tok
            b.readers = []

    def op(self, eng, fn, reads=(), writes=(), inc=True):
        waits = self._waits(eng, self._deps(eng, reads, writes))
        if inc:
            self.cnt[eng] += 1
            tok = ("e", eng, self.cnt[eng])
            self.prog[eng].append((waits, fn, (self.sem[eng], 1)))
        else:
            tok = ("e", eng, self.cnt[eng] + 1)
            self.prog[eng].append((waits, fn, None))
        self._mark(tok, reads, writes)

    def dma(self, q, semname, pairs, reads=(), writes=()):
        if semname not in self.dsem:
            self.dsem[semname] = [self.stack.enter_context(self.nc.semaphore("d_" + semname)), 0]
        waits = self._waits(q, self._deps(q, reads, writes))
        ent = self.dsem[semname]
        for i, (o, a) in enumerate(pairs):
            ent[1] += 16
            self.prog[q].append((waits if i == 0 else [],
                                 (lambda e, o=o, a=a: e.dma_start(out=o, in_=a)), (ent[0], 16)))
        tok = ("d", semname, ent[1])
        self._mark(tok, reads, writes)

    def inherit(self, new_bufs, old_bufs):
        toks = []
        for b in old_bufs:
            if b.writer is not None:
                toks.append(b.writer)
            toks.extend(b.readers)
        for b in new_bufs:
            b.readers = list(toks)

    def final_wait(self, eng, bufs):
        deps = {}
        for b in bufs:
            self._need(eng, deps, b.writer)
            for r in b.readers:
                self._need(eng, deps, r)
        w = self._waits(eng, deps)
        if w:
            self.prog[eng].append((w, None, None))

    @staticmethod
    def _replay(prog, e):
        for waits, fn, inc in prog:
            for (s, c) in waits:
                e.wait_ge(s, c)
            if fn is not None:
                ins = fn(e)
                if inc is not None:
                    ins.then_inc(inc[0], inc[1])

    def emit(self):
        with self.nc.Block() as block:
            @block.tensor
            def _(e):
                Sched._replay(self.prog["pe"], e)

            @block.scalar
            def _(e):
                Sched._replay(self.prog["act"], e)

            @block.vector
            def _(e):
                Sched._replay(self.prog["dve"], e)

            @block.gpsimd
            def _(e):
                Sched._replay(self.prog["pool"], e)

            @block.sync
            def _(e):
                Sched._replay(self.prog["sp"], e)


def _consts():
    c = {}
    gam = 1.0 - 2.0 ** (-5.0 - np.arange(4, dtype=np.float64))
    lg = np.log(gam)
    s = 128.0 ** -0.5
    j = np.arange(128)[:, None]
    i = np.arange(128)[None, :]
    mT = np.zeros((128, 4, 128), np.float64)
    for h in range(4):
        mT[:, h, :] = np.where(i >= j, np.exp(lg[h] * np.maximum(i - j, 0)), 0.0) * s
    c["maskT"] = mT.astype(np.float32)
    dq = np.zeros((128, 4, 128), np.float64)
    for h in range(4):
        dq[:, h, :] = np.exp(lg[h] * (np.arange(128) + 1.0))[None, :]
    c["dq"] = dq.astype(np.float32)
    dk = np.zeros((128, 4), np.float64)
    for h in range(4):
        dk[:, h] = np.exp(lg[h] * (127.0 - np.arange(128))) * s
    c["dk"] = dk.astype(np.float32)
    c["dchunk"] = [float(np.float32(np.exp(lg[h] * 128.0))) for h in range(4)]
    half = 64
    invf = (10000.0 ** (-np.arange(half, dtype=np.float32) / half)).astype(np.float32)
    c["invf"] = np.broadcast_to(invf[None, :], (128, 64)).copy()
    c["tril"] = np.tril(np.ones((128, 128), np.float32))
    c["identf"] = np.eye(128, dtype=np.float32)
    c["identb"] = np.eye(128, dtype=np.float32).astype(ml_dtypes.bfloat16)
    c["onesd"] = np.full((128, 128), 1.0 / D, np.float32).astype(ml_dtypes.bfloat16)
    pm = np.zeros((128, 3, 4, 128), np.float32)
    invc = np.zeros((128, 4, 128), np.float32)
    sidx = np.arange(128)[:, None]
    tidx = np.arange(128)[None, :]
    for wi, w in enumerate(POOL_W):
        cur = ((sidx <= tidx) & (sidx > tidx - w)).astype(np.float32)
        cur = cur - w * (sidx == tidx)
        pm[:, 0, wi, :] = cur
        pm[:, 1, wi, :] = (sidx - 128 > tidx - w).astype(np.float32)
        cnt = np.minimum(tidx + 1, w).astype(np.float32)
        first = ((sidx <= tidx) & (sidx > tidx - w)).astype(np.float32) - cnt * (sidx == tidx)
        pm[:, 2, wi, :] = first
        invc[:, wi, :] = 1.0 / cnt
    c["poolm"] = pm.astype(ml_dtypes.bfloat16)
    c["invc"] = invc
    return c


def build_program():
    nc = bass.Bass("TRN2", target_bir_lowering=False)
    K = _consts()

    def din(name, shape, dt=F32):
        return nc.dram_tensor(name, list(shape), dt, kind="ExternalInput").ap()

    x_d = din("x", [SEQ, D])
    c_d = din("c_col", [128, 8])
    pos_d = din("pos", [128, 16], I32)
    wada_d = din("w_ada", [2, D, 6 * D])
    bada_d = din("b_adaT", [128, 2, 48])
    n1_d = din("norm1T", [128, 2, 8])
    n2_d = din("norm2T", [128, 2, 8])
    fn_d = din("fnT", [128, 8])
    win_d = din("w_in", [2, D, DIN])
    ws_d = din("ws_gmlp", [2, 4, 128, 128])
    bs_d = din("bsT", [128, 2, 4])
    vg_d = din("vg_bc", [2, 128, 1024])
    wpool_d = din("w_pool", [2, 4, 256, 256])
    bp_d = din("b_poolT", [128, 2, 8])
    psc_d = din("pscaleT", [128, 2, 8])
    wbr_d = din("w_branch", [2, 3, D, D])
    wout_d = din("w_out", [2, D, D])
    wff1_d = din("w_ff1", [2, D, DFF])
    wff2_d = din("w_ff2", [2, DFF, D])
    maskT_d = din("maskT", [128, 4, 128])
    dq_d = din("dq", [128, 4, 128])
    dk_d = din("dk", [128, 4])
    invf_d = din("invf", [128, 64])
    tril_d = din("tril", [128, 128])
    identf_d = din("identf", [128, 128])
    identb_d = din("identb", [128, 128], BF16)
    onesd_d = din("onesd", [128, 128], BF16)
    poolm_d = din("poolm", [128, 3, 4, 128], BF16)
    invc_d = din("invc", [128, 4, 128])
    out_d = nc.dram_tensor("out", [SEQ, D], F32, kind="ExternalOutput").ap()

    with contextlib.ExitStack() as st:
        S = Sched(nc, st)

        def sb(name, shape, dt=F32):
            return st.enter_context(nc.sbuf_tensor(name, list(shape), dt))

        xT = sb("xT", [128, 8, TG])
        XB = [[Buf(f"x{k}{h}") for h in range(2)] for k in range(8)]
        hT = sb("hT", [128, 8, TG], BF16)
        HB = [[Buf(f"h{k}{h}") for h in range(2)] for k in range(8)]
        arena = sb("arena", [128, 24576], BF16)
        AR = [Buf(f"ar{i}") for i in range(24)]
        merged = arena[:, 0:16384].bitcast(F32).rearrange("p (k t) -> p k t", k=8)
        ynT = arena[:, 16384:24576].rearrange("p (k t) -> p k t", k=8)
        gv = arena[:, 0:8192].rearrange("p (c f) -> p c f", c=8)
        hid = [arena[:, 0:8192].rearrange("p (k t) -> p k t", k=8),
               arena[:, 8192:16384].rearrange("p (k t) -> p k t", k=8)]

        def MB(kd, half):
            return AR[2 * kd + half]

        def YB(kc):
            return AR[16 + kc]

        NSLOT = 4
        wslot = [sb(f"wslot{i}", [128, 4096], BF16) for i in range(NSLOT)]
        WS = [Buf(f"ws{i}") for i in range(NSLOT)]
        wctr = [0]

        cos_t = sb("cos_t", [128, 16, 64]); sin_t = sb("sin_t", [128, 16, 64]); CS = Buf("cs")
        Sst = sb("Sst", [128, 2, 4, 256]); Sbf = sb("Sbf", [128, 2, 4, 256], BF16)
        SB_ = [[Buf(f"S{l}{h}") for h in range(4)] for l in range(2)]
        SBF = [[Buf(f"Sbf{l}{h}") for h in range(4)] for l in range(2)]
        modT = sb("modT", [128, 2, 48]); MOD = [Buf("mod0"), Buf("mod1")]
        g12 = sb("g12", [128, 2, 2, 8])
        n1T = sb("n1T", [128, 2, 8]); n2T = sb("n2T", [128, 2, 8]); fnT = sb("fnT_s", [128, 8])
        badaT = sb("badaT", [128, 2, 48])
        bsT = sb("bsT_s", [128, 2, 4]); bpT = sb("bpT", [128, 2, 8]); pscT = sb("pscT", [128, 2, 8])
        SMALL = Buf("small")
        bps = sb("bps", [128, 2, 8])
        vg = sb("vg", [128, 2, 1024]); VG = Buf("vg")
        maskT = sb("maskT_s", [128, 4, 128]); dq = sb("dq_s", [128, 4, 128]); dk = sb("dk_s", [128, 4])
        invc = sb("invc_s", [128, 4, 128])
        tril = sb("tril_s", [128, 128])
        identf = sb("identf_s", [128, 128]); identb = sb("identb_s", [128, 128], BF16)
        onesd = sb("onesd_s", [128, 128], BF16)
        poolm = sb("poolm_s", [128, 3, 4, 128], BF16)
        CONST = Buf("const")
        WsT = sb("WsT", [128, 2, 4, 128], BF16); WST = Buf("wst")
        wp = sb("wp", [128, 4, 2, 256], BF16); WP = Buf("wp")
        halo = sb("halo", [128, 2, 1024], BF16); HALO = [Buf("halo0"), Buf("halo1")]
        eps_t = sb("eps_t", [128, 1])
        cact = sb("cact", [128, 8], BF16); CACT = Buf("cact")
        temp = sb("temp", [128, 14592], BF16)
        tstate = {"bufs": []}

        ps_t = [st.enter_context(nc.psum_tensor(f"psb{i}", [128, 512], F32)) for i in range(8)]
        PB = [Buf(f"pb{i}", excl=True) for i in range(8)]

        def temp_phase(spec):
            off = 0
            views = {}
            bufs = {}
            newb = []
            for name, shape, dt in spec:
                n = int(np.prod(shape[1:]))
                nb16 = n * (2 if dt in (F32, I32) else 1)
                v = temp[:, off:off + nb16]
                if dt != BF16:
                    v = v.bitcast(dt)
                if len(shape) == 3:
                    v = v.rearrange("p (a b) -> p a b", a=shape[1])
                elif len(shape) == 4:
                    v = v.rearrange("p (a b c) -> p a b c", a=shape[1], b=shape[2])
                views[name] = v
                bufs[name] = Buf("t_" + name)
                newb.append(bufs[name])
                off += nb16
            assert off <= 14592, off
            S.inherit(newb, tstate["bufs"])
            tstate["bufs"] = newb
            return views, bufs

        def mm_group(out_ap, pairs, reads, writes):
            n = len(pairs)
            for i, (l, r) in enumerate(pairs):
                S.op("pe", (lambda e, l=l, r=r, i=i: e.matmul(out_ap, lhsT=l, rhs=r, start=(i == 0), stop=(i == n - 1))),
                     reads=reads, writes=writes, inc=(i == n - 1))

        def wload(parts, kd, width):
            s = wctr[0] % NSLOT
            wctr[0] += 1
            view = wslot[s][:, 0:kd * width].rearrange("p (k w) -> p k w", k=kd)
            S.dma("pool", f"w{s}", [(view[:, :, c0:c0 + w], ap) for (c0, w, ap) in parts], writes=[WS[s]])
            return view, WS[s]

        def wcols(dram2d, c0, w):
            return dram2d[:, c0:c0 + w].rearrange("(k p) w -> p k w", p=128)

        def act(out, in_, func, reads, writes, bias=None, scale=None, accum=None):
            kw = {}
            if bias is not None:
                kw["bias"] = bias
            if scale is not None:
                kw["scale"] = scale
            if accum is not None:
                kw["accum_out"] = accum
            S.op("act", (lambda e: e.activation(out, in_, func, **kw)), reads=reads, writes=writes)

        def tt(out, a, b, op, reads, writes, eng="dve"):
            S.op(eng, (lambda e: e.tensor_tensor(out, a, b, op=op)), reads=reads, writes=writes)

        def tsc(out, a, s1, s2, op0, op1, reads, writes, eng="dve"):
            if op1 is None:
                S.op(eng, (lambda e: e.tensor_scalar(out, a, s1, None, op0=op0)), reads=reads, writes=writes)
            else:
                S.op(eng, (lambda e: e.tensor_scalar(out, a, s1, s2, op0=op0, op1=op1)), reads=reads, writes=writes)

        def stt(out, a, sc, b, op0, op1, reads, writes, accum=None, eng="dve"):
            if accum is None:
                S.op(eng, (lambda e: e.scalar_tensor_tensor(out, a, sc, b, op0=op0, op1=op1)), reads=reads, writes=writes)
            else:
                S.op(eng, (lambda e: e.scalar_tensor_tensor(out, a, sc, b, op0=op0, op1=op1, accum_out=accum)),
                     reads=reads, writes=writes)

        def cp(eng, out, in_, reads, writes):
            if eng == "act":
                S.op("act", (lambda e: e.copy(out, in_)), reads=reads, writes=writes)
            else:
                S.op(eng, (lambda e: e.tensor_copy(out, in_)), reads=reads, writes=writes)

        def transpose(out, in_, ident, reads, writes, inc=True):
            S.op("pe", (lambda e: e.transpose(out, in_, ident)), reads=reads, writes=writes, inc=inc)

        def psb(i):
            return ps_t[i][:].bitcast(BF16)

        S.dma("sp", "c0", [(maskT[:], maskT_d), (dq[:], dq_d), (dk[:], dk_d), (invc[:], invc_d),
                           (identf[:], identf_d), (identb[:], identb_d), (onesd[:], onesd_d),
                           (poolm[:], poolm_d), (tril[:], tril_d)], writes=[CONST])
        S.dma("sp", "c1", [(badaT[:], bada_d), (n1T[:], n1_d), (n2T[:], n2_d), (fnT[:], fn_d),
                           (bsT[:], bs_d), (bpT[:], bp_d), (pscT[:], psc_d)], writes=[SMALL])
        S.dma("sp", "c2", [(vg[:, l, :], vg_d[l]) for l in range(2)], writes=[VG])
        S.op("dve", lambda e: e.memset(eps_t[:], EPS), writes=[CONST])
        tt(bps[:], bpT[:], pscT[:], ALU.mult, [SMALL], [SMALL])
        for l in range(2):
            for h in range(4):
                S.op("dve", lambda e, l=l, h=h: e.memset(Sst[:, l, h, :], 0.0), writes=[SB_[l][h]])
                S.op("dve", lambda e, l=l, h=h: e.memset(Sbf[:, l, h, :], 0.0), writes=[SBF[l][h]])

        tv, tb = temp_phase([("cc", [128, 8], F32), ("posi", [128, 16], I32), ("posf", [128, 16], F32),
                             ("invf", [128, 64], F32), ("ang", [128, 16, 64], F32), ("y", [128, 16, 64], F32),
                             ("ki", [128, 16, 64], I32), ("kf", [128, 16, 64], F32), ("m", [128, 16, 64], F32),
                             ("wsf", [128, 4, 128], F32), ("wsb", [128, 4, 128], BF16)])
        S.dma("sp", "c3", [(tv["cc"], c_d), (tv["posi"], pos_d), (tv["invf"], invf_d)],
              writes=[tb["cc"], tb["posi"], tb["invf"]])
        act(cact[:], tv["cc"], AF.Silu, [tb["cc"]], [CACT])

        cp("dve", tv["posf"], tv["posi"], [tb["posi"]], [tb["posf"]])
        tt(tv["ang"], tv["posf"].unsqueeze(2).broadcast_to([128, 16, 64]),
           tv["invf"].unsqueeze(1).broadcast_to([128, 16, 64]), ALU.mult, [tb["posf"], tb["invf"]], [tb["ang"]])
        TWO_PI = 2.0 * np.pi
        C1 = 6.28125
        C2 = float(TWO_PI - C1)
        for which, dst in (("sin", sin_t), ("cos", cos_t)):
            shift = 0.0 if which == "sin" else float(np.pi / 2)
            tsc(tv["y"], tv["ang"], shift, None, ALU.add, None, [tb["ang"]], [tb["y"]])
            tsc(tv["ki"], tv["y"], float(1.0 / TWO_PI), None, ALU.mult, None, [tb["y"]], [tb["ki"]])
            cp("dve", tv["kf"], tv["ki"], [tb["ki"]], [tb["kf"]])
            stt(tv["y"], tv["kf"], -C1, tv["y"], ALU.mult, ALU.add, [tb["kf"], tb["y"]], [tb["y"]])
            stt(tv["y"], tv["kf"], -C2, tv["y"], ALU.mult, ALU.add, [tb["kf"], tb["y"]], [tb["y"]])
            tsc(tv["m"], tv["y"], float(np.pi), float(-TWO_PI), ALU.is_gt, ALU.mult, [tb["y"]], [tb["m"]])
            tt(tv["y"], tv["y"], tv["m"], ALU.add, [tb["y"], tb["m"]], [tb["y"]])
            tsc(tv["m"], tv["y"], float(-np.pi), float(TWO_PI), ALU.is_lt, ALU.mult, [tb["y"]], [tb["m"]])
            tt(tv["y"], tv["y"], tv["m"], ALU.add, [tb["y"], tb["m"]], [tb["y"]])
            tsc(tv["y"], tv["y"], float(np.pi), float(-np.pi), ALU.min, ALU.max, [tb["y"]], [tb["y"]])
            act(dst[:], tv["y"], AF.Sin, [tb["y"]], [CS])


        for l in range(2):
            S.dma("sp", "c4", [(tv["wsf"], ws_d[l].rearrange("g t s -> t g s"))], writes=[tb["wsf"]])
            tt(tv["wsb"], tv["wsf"], tril[:].unsqueeze(1).broadcast_to([128, 4, 128]), ALU.mult,
               [tb["wsf"], CONST], [tb["wsb"]])
            for g in range(4):
                transpose(psb(7)[:, g * 128:(g + 1) * 128], tv["wsb"][:, g, :], identb[:],
                          [tb["wsb"], CONST], [PB[7]], inc=(g == 3))
            cp("dve", WsT[:, l, :, :], psb(7)[:, 0:512].rearrange("p (g t) -> p g t", g=4), [PB[7]], [WST])
        for l in range(2):
            S.op("dve", lambda e, l=l: e.memset(halo[:, l, :], 0.0), writes=[HALO[l]])

        def mod_block(l, blk):
            wv, wb = wload([(0, 512, wcols(wada_d[l], blk * 512, 512))], 8, 512)
            for j in range(4):
                col = blk * 4 + j
                mm_group(ps_t[6][:, col:col + 1],
                         [(wv[:, kd, j * 128:(j + 1) * 128], cact[:, kd:kd + 1]) for kd in range(8)],
                         [wb, CACT], [PB[6]])

        def mod_finish(l):
            tt(modT[:, l, :], ps_t[6][:, 0:48], badaT[:, l, :], ALU.add, [PB[6], SMALL], [MOD[l]])
            stt(g12[:, l, 0, :], modT[:, l, 8:16], 1.0, n1T[:, l, :], ALU.add, ALU.mult, [MOD[l], SMALL], [MOD[l]])
            stt(g12[:, l, 1, :], modT[:, l, 32:40], 1.0, n2T[:, l, :], ALU.add, ALU.mult, [MOD[l], SMALL], [MOD[l]])

        def compute_mod(l):
            for blk in range(12):
                mod_block(l, blk)
            mod_finish(l)

        def norm_to_h(gcol, shcol, modbuf):
            tv, tb = temp_phase([("sq0", [128, 8, 512], BF16), ("sq1", [128, 8, 512], BF16),
                                 ("rt0", [128, 512], F32), ("rt1", [128, 512], F32),
                                 ("rstd0", [128, 512], F32), ("rstd1", [128, 512], F32),
                                 ("t0", [128, 512], F32), ("t1", [128, 512], F32)])
            for kd in range(8):
                tt(tv["sq0"][:, kd, :], xT[:, kd, 0:512], xT[:, kd, 0:512], ALU.mult, [XB[kd][0]], [tb["sq0"]])
                act(tv["sq1"][:, kd, :], xT[:, kd, 512:1024], AF.Square, [XB[kd][1]], [tb["sq1"]])
            for half in range(2):
                sq, rt, pbi = f"sq{half}", f"rt{half}", 4 + half
                mm_group(ps_t[pbi][:], [(onesd[:], tv[sq][:, kd, :]) for kd in range(8)], [tb[sq], CONST], [PB[pbi]])
                act(tv[rt], ps_t[pbi][:], AF.Sqrt, [PB[pbi], CONST], [tb[rt]], bias=eps_t[:, 0:1], scale=1.0)
            for half in range(2):
                hs = slice(half * 512, (half + 1) * 512)
                rt, rstd = f"rt{half}", f"rstd{half}"
                S.op("dve", lambda e, rt=rt, rstd=rstd: e.reciprocal(tv[rstd], tv[rt]), reads=[tb[rt]], writes=[tb[rstd]])
                for kd in range(8):
                    t = "t0" if kd % 2 == 0 else "t1"
                    tt(tv[t], xT[:, kd, hs], tv[rstd], ALU.mult, [XB[kd][half], tb[rstd]], [tb[t]])
                    act(hT[:, kd, hs], tv[t], AF.Identity, [tb[t], modbuf], [HB[kd][half]],
                        bias=shcol(kd), scale=gcol(kd))

        def branch(l, n, first):
            tv, tb = temp_phase([("sg0", [128, 512], F32), ("sg1", [128, 512], F32),
                                 ("pr0", [128, 512], F32), ("pr1", [128, 512], F32)])
            it = 0
            for mb in range(2):
                wa, wab = wload([(0, 512, wcols(wbr_d[l, n], mb * 512, 512))], 8, 512)
                wg, wgb = wload([(0, 512, wcols(win_d[l], OFF_GATE + n * 1024 + mb * 512, 512))], 8, 512)
                for mi in range(4):
                    m = mb * 4 + mi
                    for half in range(2):
                        hs = slice(half * 512, (half + 1) * 512)
                        bpb, gtb = (0, 1) if it % 2 == 0 else (2, 3)
                        mm_group(ps_t[bpb][:], [(wa[:, kc, mi * 128:(mi + 1) * 128], ynT[:, kc, hs]) for kc in range(8)],
                                 [wab] + [YB(kc) for kc in range(8)], [PB[bpb]])
                        mm_group(ps_t[gtb][:], [(wg[:, kd, mi * 128:(mi + 1) * 128], hT[:, kd, hs]) for kd in range(8)],
                                 [wgb] + [HB[kd][half] for kd in range(8)], [PB[gtb]])
                        sg = "sg0" if it % 2 == 0 else "sg1"
                        pr = "pr0" if it % 2 == 0 else "pr1"
                        act(tv[sg], ps_t[gtb][:], AF.Sigmoid, [PB[gtb]], [tb[sg]])
                        if first:
                            tt(merged[:, m, hs], ps_t[bpb][:], tv[sg], ALU.mult, [PB[bpb], tb[sg]], [MB(m, half)])
                        else:
                            tt(tv[pr], ps_t[bpb][:], tv[sg], ALU.mult, [PB[bpb], tb[sg]], [tb[pr]])
                            tt(merged[:, m, hs], merged[:, m, hs], tv[pr], ALU.add, [tb[pr], MB(m, half)], [MB(m, half)])
                        it += 1

        def sgu(l):
            tv, tb = temp_phase([("ss", [128, 16], F32), ("ss8", [128, 8], F32), ("rt8", [128, 8], F32),
                                 ("rstd8", [128, 8], F32), ("junk", [128, 512], BF16),
                                 ("u0", [128, 512], BF16), ("u1", [128, 512], BF16),
                                 ("gn0", [128, 512], BF16), ("gn1", [128, 512], BF16),
                                 ("y0", [128, 512], BF16), ("y1", [128, 512], BF16)]
                                + [(f"gv{c}", [128, 1024], BF16) for c in range(NCH)])
            S.op("dve", lambda e: e.memset(tv["ss"], 0.0), writes=[tb["ss"]])
            for blk in range(2):
                wv, wb = wload([(0, 512, wcols(win_d[l], OFF_VS + blk * 512, 512))], 8, 512)
                for c in range(NCH):
                    b = c % 2
                    cs = slice(c * 128, (c + 1) * 128)
                    mm_group(ps_t[b][:], [(hT[:, kd, cs], wv[:, kd, :]) for kd in range(8)],
                             [wb] + [HB[kd][c // 4] for kd in range(8)], [PB[b]])
                    gvc = tv[f"gv{c}"][:, blk * 512:(blk + 1) * 512]
                    act(gvc, ps_t[b][:], AF.Gelu_apprx_tanh, [PB[b]], [tb[f"gv{c}"]])
                    stt(tv["junk"], gvc, 1.0, gvc,
                        ALU.mult, ALU.mult, [tb[f"gv{c}"]], [tb["junk"], tb["ss"]], accum=tv["ss"][:, c * 2 + blk:c * 2 + blk + 1])
            ssv = tv["ss"].rearrange("p (c b) -> p c b", b=2)
            tt(tv["ss8"], ssv[:, :, 0], ssv[:, :, 1], ALU.add, [tb["ss"]], [tb["ss8"]])
            act(tv["rt8"], tv["ss8"], AF.Sqrt, [tb["ss8"], CONST], [tb["rt8"]], bias=eps_t[:, 0:1], scale=1.0 / 1024.0)
            S.op("dve", lambda e: e.reciprocal(tv["rstd8"], tv["rt8"]), reads=[tb["rt8"]], writes=[tb["rstd8"]])
            for blk in range(2):
                wv, wb = wload([(0, 512, wcols(win_d[l], OFF_U + blk * 512, 512))], 8, 512)
                fs = slice(blk * 512, (blk + 1) * 512)

                def sA(c, blk=blk, wv=wv, wb=wb, fs=fs):
                    b = c % 2
                    cs = slice(c * 128, (c + 1) * 128)
                    u, gn = f"u{b}", f"gn{b}"
                    mm_group(ps_t[b][:], [(hT[:, kd, cs], wv[:, kd, :]) for kd in range(8)],
                             [wb] + [HB[kd][c // 4] for kd in range(8)], [PB[b]])
                    act(tv[u], ps_t[b][:], AF.Gelu_apprx_tanh, [PB[b]], [tb[u]])
                    stt(tv[gn], tv[f"gv{c}"][:, fs], tv["rstd8"][:, c:c + 1], vg[:, l, fs], ALU.mult, ALU.mult,
                        [tb[f"gv{c}"], tb["rstd8"], VG], [tb[gn]])

                def sB(c, blk=blk):
                    b = c % 2
                    u, gn, y = f"u{b}", f"gn{b}", f"y{b}"
                    for gg in range(2):
                        g = 2 * blk + gg
                        S.op("pe", (lambda e, b=b, gg=gg, g=g, gn=gn: e.matmul(
                            ps_t[2 + b][:, gg * 256:(gg + 1) * 256], lhsT=WsT[:, l, g, :],
                            rhs=tv[gn][:, gg * 256:(gg + 1) * 256], start=True, stop=True)),
                            reads=[WST, tb[gn]], writes=[PB[2 + b]], inc=(gg == 1))
                    for gg in range(2):
                        g = 2 * blk + gg
                        stt(tv[y][:, gg * 256:(gg + 1) * 256], ps_t[2 + b][:, gg * 256:(gg + 1) * 256],
                            bsT[:, l, g:g + 1], tv[u][:, gg * 256:(gg + 1) * 256], ALU.add, ALU.mult,
                            [PB[2 + b], SMALL, tb[u]], [tb[y]])

                def sC(c, blk=blk):
                    b = c % 2
                    cs = slice(c * 128, (c + 1) * 128)
                    y = f"y{b}"
                    for j in range(4):
                        transpose(psb(4 + b)[:, j * 128:(j + 1) * 128], tv[y][:, j * 128:(j + 1) * 128], identb[:],
                                  [tb[y], CONST], [PB[4 + b]], inc=(j == 3))
                    cp("act", ynT[:, 4 * blk:4 * blk + 4, cs],
                       psb(4 + b)[:, 0:512].rearrange("p (k t) -> p k t", k=4),
                       [PB[4 + b]], [YB(4 * blk + j) for j in range(4)])

                for step in range(NCH + 2):
                    if 0 <= step - 2 < NCH:
                        sC(step - 2)
                    if 0 <= step - 1 < NCH:
                        sB(step - 1)
                    if step < NCH:
                        sA(step)

        def retention(l, grp):
            tv, tb = temp_phase([("ss0", [128, 8], F32), ("ss1", [128, 8], F32), ("rt", [128, 8], F32), ("rstd", [128, 8], F32),
                                 ("junk", [128, 256], BF16),
                                 ("a0", [128, 4, 64], F32), ("a1", [128, 4, 64], F32),
                                 ("b10", [128, 2, 64], F32), ("b11", [128, 2, 64], F32),
                                 ("b20", [128, 2, 64], F32), ("b21", [128, 2, 64], F32),
                                 ("qk0", [128, 256], BF16), ("qk1", [128, 256], BF16),
                                 ("xq0", [128, 256], F32), ("xq1", [128, 256], F32),
                                 ("v0", [128, 256], BF16), ("v1", [128, 256], BF16), ("v2", [128, 256], BF16), ("v3", [128, 256], BF16),
                                 ("qkT0", [128, 3, 128], BF16), ("qkT1", [128, 3, 128], BF16),
                                 ("kd0", [128, 128], BF16), ("kd1", [128, 128], BF16), ("kd2", [128, 128], BF16),
                                 ("sT0", [128, 128], BF16), ("sT1", [128, 128], BF16)])
            o_v = [arena[:, p * 4096:(p + 1) * 4096].bitcast(F32).rearrange("p (c e) -> p c e", c=8) for p in range(2)]
            g_v = [arena[:, 8192 + p * 2048:8192 + (p + 1) * 2048].rearrange("p (c e) -> p c e", c=8) for p in range(2)]
            ytm_v = arena[:, 12288:14336].rearrange("p (c e) -> p c e", c=8)
            OB = [Buf("ret_o0"), Buf("ret_o1")]
            GB = [Buf("ret_g0"), Buf("ret_g1")]
            YTB = Buf("ret_ytm")
            S.inherit(OB + GB + [YTB], AR[0:16])
            wts = {}

            def names_it(it):
                b = it % 2
                return (b, f"a{b}", f"b1{b}", f"b2{b}", f"qk{b}", f"v{it % 4}", f"qkT{b}", f"kd{it % 3}", f"sT{b}", f"xq{b}")

            cur_h = [0]

            def names(c):
                return names_it(cur_h[0] * NCH + c)

            def stA1(h, c):
                if c == 0:
                    wa, wab = wload([(0, 512, wcols(win_d[l], 512 * h, 512))], 8, 512)
                    wg, wgb = wload([(0, 256, wcols(win_d[l], OFF_G + 256 * h, 256))], 8, 256)
                    wts[h] = (wa, wab, wg, wgb)
                    S.op("dve", lambda e, h=h: e.memset(tv[f"ss{h % 2}"], 0.0), writes=[tb[f"ss{h % 2}"]])
                wa, wab, wg, wgb = wts[h]
                b, a, b1, b2, qk, v, qkT, kdt, sT, xq = names(c)
                cs = slice(c * 128, (c + 1) * 128)
                hb = [HB[kd][c // 4] for kd in range(8)]
                QKV, G = ps_t[b], ps_t[2 + b]
                mm_group(QKV[:], [(hT[:, kd, cs], wa[:, kd, :]) for kd in range(8)], [wab] + hb, [PB[b]])
                mm_group(G[:, 0:256], [(hT[:, kd, cs], wg[:, kd, :]) for kd in range(8)], [wgb] + hb, [PB[2 + b]])
                cp("act", tv[xq], QKV[:, 0:256], [PB[b]], [tb[xq]])
                cp("act", tv[v], QKV[:, 256:512], [PB[b]], [tb[v]])
                act(g_v[h % 2][:, c, :], G[:, 0:256], AF.Silu, [PB[2 + b]], [GB[h % 2]])

            def stA2(h, c):
                b, a, b1, b2, qk, v, qkT, kdt, sT, xq = names(c)
                n = grp * NCH + c
                q4 = tv[xq].rearrange("p (a f) -> p a f", a=4)
                q22 = tv[xq].rearrange("p (a b f) -> p a b f", a=2, b=2)
                cosb = cos_t[:, n, :].unsqueeze(1).broadcast_to([128, 4, 64])
                sinb = sin_t[:, n, :].unsqueeze(1).broadcast_to([128, 2, 64])
                tt(tv[a], q4, cosb, ALU.mult, [tb[xq], CS], [tb[a]])
                tt(tv[b1], q22[:, :, 1, :], sinb, ALU.mult, [tb[xq], CS], [tb[b1]])
                tt(tv[b2], q22[:, :, 0, :], sinb, ALU.mult, [tb[xq], CS], [tb[b2]])
                a22 = tv[a].rearrange("p (a b) f -> p a b f", b=2)
                qk22 = tv[qk].rearrange("p (a b f) -> p a b f", a=2, b=2)
                stt(qk22[:, :, 0, :], tv[b1], -1.0, a22[:, :, 0, :], ALU.mult, ALU.add, [tb[a], tb[b1]], [tb[qk]])
                tt(qk22[:, :, 1, :], a22[:, :, 1, :], tv[b2], ALU.add, [tb[a], tb[b2]], [tb[qk]])
                tt(tv[kdt], tv[qk][:, 128:256], dk[:, h:h + 1].broadcast_to([128, 128]), ALU.mult,
                   [tb[qk], CONST], [tb[kdt]])

            def stB1(h, c):
                b, a, b1, b2, qk, v, qkT, kdt, sT, xq = names(c)
                transpose(psb(4)[:, 0:128], tv[qk][:, 0:128], identb[:], [tb[qk], CONST], [PB[4]], inc=False)
                transpose(psb(4)[:, 128:256], tv[qk][:, 128:256], identb[:], [tb[qk], CONST], [PB[4]])
                cp("act", tv[qkT][:, 0, :], psb(4)[:, 0:128], [PB[4]], [tb[qkT]])
                cp("act", tv[qkT][:, 2, :], psb(4)[:, 128:256], [PB[4]], [tb[qkT]])
                tt(tv[qkT][:, 1, :], tv[qkT][:, 0, :], dq[:, h, :], ALU.mult, [tb[qkT], CONST], [tb[qkT]])

            def stB2(h, c):
                b, a, b1, b2, qk, v, qkT, kdt, sT, xq = names(c)
                S.op("pe", (lambda e, qkT=qkT: e.matmul(ps_t[5][:, 0:128], lhsT=tv[qkT][:, 2, :], rhs=tv[qkT][:, 0, :],
                                                        start=True, stop=True)), reads=[tb[qkT]], writes=[PB[5]])
                tt(tv[sT], ps_t[5][:, 0:128], maskT[:, h, :], ALU.mult, [PB[5], CONST], [tb[sT]])

            def tail1(h):
                hp = h % 2
                ss = f"ss{hp}"
                act(tv["rt"], tv[ss], AF.Sqrt, [tb[ss], CONST], [tb["rt"]], bias=eps_t[:, 0:1], scale=1.0 / 256.0)
                S.op("dve", lambda e: e.reciprocal(tv["rstd"], tv["rt"]), reads=[tb["rt"]], writes=[tb["rstd"]])
                for c in range(NCH):
                    stt(ytm_v[:, c, :], o_v[hp][:, c, :], tv["rstd"][:, c:c + 1], g_v[hp][:, c, :], ALU.mult, ALU.mult,
                        [OB[hp], tb["rstd"], GB[hp]], [YTB])

            def tail2(h):
                for c4 in range(2):
                    pbi = 6 + c4
                    for cc in range(4):
                        for e2 in range(2):
                            transpose(psb(pbi)[:, (cc * 2 + e2) * 128:(cc * 2 + e2 + 1) * 128],
                                      ytm_v[:, c4 * 4 + cc, e2 * 128:(e2 + 1) * 128], identb[:],
                                      [YTB, CONST], [PB[pbi]], inc=(cc == 3 and e2 == 1))
                    cp("act", ynT[:, 2 * h:2 * h + 2, c4 * 512:(c4 + 1) * 512].rearrange("p e (c t) -> p e c t", c=4),
                       psb(pbi)[:, 0:1024].rearrange("p (c e t) -> p e c t", c=4, e=2),
                       [PB[pbi]], [YB(2 * h), YB(2 * h + 1)])

            pending_tail = {}

            def stC(h, c):
                b, a, b1, b2, qk, v, qkT, kdt, sT, xq = names(c)
                hp = h % 2
                S.op("pe", (lambda e, sT=sT, v=v: e.matmul(ps_t[7][:, 0:256], lhsT=tv[sT], rhs=tv[v], start=True, stop=False)),
                     reads=[tb[sT], tb[v]], writes=[PB[7]], inc=False)
                S.op("pe", (lambda e, qkT=qkT, h=h: e.matmul(ps_t[7][:, 0:256], lhsT=tv[qkT][:, 1, :], rhs=Sbf[:, l, h, :],
                                                             start=False, stop=True)),
                     reads=[tb[qkT], SBF[l][h]], writes=[PB[7]])
                S.op("pe", (lambda e, kdt=kdt, v=v: e.matmul(ps_t[6][:, 0:256], lhsT=tv[kdt], rhs=tv[v], start=True, stop=True)),
                     reads=[tb[kdt], tb[v]], writes=[PB[6]])
                tsc(Sst[:, l, h, :], Sst[:, l, h, :], K["dchunk"][h], None, ALU.mult, None, [SB_[l][h]], [SB_[l][h]])
                tt(Sst[:, l, h, :], ps_t[6][:, 0:256], Sst[:, l, h, :], ALU.add, [PB[6], SB_[l][h]], [SB_[l][h]])
                cp("act", Sbf[:, l, h, :], Sst[:, l, h, :], [SB_[l][h]], [SBF[l][h]])
                cp("act", o_v[hp][:, c, :], ps_t[7][:, 0:256], [PB[7]], [OB[hp]])
                act(tv["junk"], o_v[hp][:, c, :], AF.Square, [OB[hp]], [tb["junk"], tb[f"ss{hp}"]],
                    accum=tv[f"ss{hp}"][:, c:c + 1])
                if c == NCH - 1:
                    tail1(h)
                    pending_tail[h] = 2

            stages = [stA1, stA2, stB1, stB2, stC]
            NIT = 4 * NCH
            for step in range(NIT + len(stages) - 1):
                for si in range(len(stages) - 1, -1, -1):
                    it = step - si
                    if 0 <= it < NIT:
                        cur_h[0] = it // NCH
                        stages[si](it // NCH, it % NCH)
                for hh in sorted(pending_tail):
                    pending_tail[hh] -= 1
                    if pending_tail[hh] < 0:
                        tail2(hh)
                        del pending_tail[hh]
            for hh in sorted(pending_tail):
                tail2(hh)
            toks = []
            for bb in OB + GB + [YTB]:
                if bb.writer is not None:
                    toks.append(bb.writer)
                toks.extend(bb.readers)
            for bb in AR[0:16]:
                bb.readers.extend(toks)

        def pool(l, grp):
            tv, tb = temp_phase([("p0", [128, 1024], BF16), ("p1", [128, 1024], BF16),
                                 ("pl0", [128, 8, 128], BF16), ("pl1", [128, 8, 128], BF16),
                                 ("pf", [128, 2, 128], F32)])
            S.dma("pool", "wp", [(wp[:], wpool_d[l].rearrange("g (cc p) e -> p g cc e", p=128))], writes=[WP])
            w0, w0b = wload([(0, 512, wcols(win_d[l], OFF_P, 512))], 8, 512)
            w1, w1b = wload([(0, 512, wcols(win_d[l], OFF_P + 512, 512))], 8, 512)
            def pA(c):
                b = c % 2
                cs = slice(c * 128, (c + 1) * 128)
                hb = [HB[kd][c // 4] for kd in range(8)]
                cur, curb = tv[f"p{b}"], tb[f"p{b}"]
                for blk, (wv, wb) in enumerate(((w0, w0b), (w1, w1b))):
                    mm_group(ps_t[blk][:], [(hT[:, kd, cs], wv[:, kd, :]) for kd in range(8)], [wb] + hb, [PB[blk]])
                    cp("act" if blk == 0 else "dve", cur[:, blk * 512:(blk + 1) * 512], ps_t[blk][:], [PB[blk]], [curb])
                if grp == 0 and c == NCH - 1:
                    cp("act", halo[:, l, :], cur, [curb], [HALO[l]])

            def pB(c):
                b = c % 2
                cur, curb = tv[f"p{b}"], tb[f"p{b}"]
                if c == 0:
                    prev, prevb = halo[:, l, :], HALO[l]
                else:
                    prev, prevb = tv[f"p{1 - b}"], tb[f"p{1 - b}"]
                first_chunk = (grp == 0 and c == 0)
                for kc in range(8):
                    gi = kc // 2
                    pbi = 2 + kc // 4
                    dst = ps_t[pbi][:, (kc % 4) * 128:(kc % 4 + 1) * 128]
                    last = (kc % 4 == 3)
                    if first_chunk:
                        S.op("pe", (lambda e, dst=dst, kc=kc, gi=gi, cur=cur: e.matmul(
                            dst, lhsT=cur[:, kc * 128:(kc + 1) * 128], rhs=poolm[:, 2, gi, :], start=True, stop=True)),
                            reads=[curb, CONST], writes=[PB[pbi]], inc=last)
                    else:
                        S.op("pe", (lambda e, dst=dst, kc=kc, gi=gi, cur=cur: e.matmul(
                            dst, lhsT=cur[:, kc * 128:(kc + 1) * 128], rhs=poolm[:, 0, gi, :], start=True, stop=False)),
                            reads=[curb, CONST], writes=[PB[pbi]], inc=False)
                        S.op("pe", (lambda e, dst=dst, kc=kc, gi=gi, prev=prev: e.matmul(
                            dst, lhsT=prev[:, kc * 128:(kc + 1) * 128], rhs=poolm[:, 1, gi, :], start=False, stop=True)),
                            reads=[prevb, CONST], writes=[PB[pbi]], inc=last)
                pl, plb = tv[f"pl{b}"], tb[f"pl{b}"]
                for gi in range(4):
                    pbi = 2 + gi // 2
                    src = ps_t[pbi][:, (gi % 2) * 256:(gi % 2 + 1) * 256].rearrange("p (k t) -> p k t", k=2)
                    if first_chunk:
                        cp("act", tv["pf"], src, [PB[pbi]], [tb["pf"]])
                        tt(pl[:, 2 * gi:2 * gi + 2, :], tv["pf"], invc[:, gi, :].unsqueeze(1).broadcast_to([128, 2, 128]),
                           ALU.mult, [tb["pf"], CONST], [plb])
                    else:
                        act(pl[:, 2 * gi:2 * gi + 2, :], src, AF.Copy, [PB[pbi]], [plb], scale=1.0 / POOL_W[gi])

            def pC(c):
                b = c % 2
                cs = slice(c * 128, (c + 1) * 128)
                pl, plb = tv[f"pl{b}"], tb[f"pl{b}"]
                for ko in range(8):
                    gi, ec = ko // 2, ko % 2
                    pbi = 4 + ko // 4
                    dst = ps_t[pbi][:, (ko % 4) * 128:(ko % 4 + 1) * 128]
                    for cc in range(2):
                        S.op("pe", (lambda e, dst=dst, gi=gi, ec=ec, cc=cc, pl=pl: e.matmul(
                            dst, lhsT=wp[:, gi, cc, ec * 128:(ec + 1) * 128], rhs=pl[:, 2 * gi + cc, :],
                            start=(cc == 0), stop=(cc == 1))),
                            reads=[WP, plb], writes=[PB[pbi]], inc=(cc == 1 and ko % 4 == 3))
                for hb4 in range(2):
                    pbi = 4 + hb4
                    src = ps_t[pbi][:].rearrange("p (k t) -> p k t", k=4)
                    for j in range(4):
                        ko = hb4 * 4 + j
                        act(ynT[:, ko, cs], src[:, j, :], AF.Identity, [PB[pbi], SMALL], [YB(ko)],
                            bias=bps[:, l, ko:ko + 1], scale=pscT[:, l, ko:ko + 1])

            for step in range(NCH + 2):
                if 0 <= step - 2 < NCH:
                    pC(step - 2)
                if 0 <= step - 1 < NCH:
                    pB(step - 1)
                if step < NCH:
                    pA(step)

        def out_proj(l):
            mbf = ynT
            for kd in range(8):
                cp("act" if kd % 2 == 0 else "dve", mbf[:, kd, :], merged[:, kd, :], [MB(kd, 0), MB(kd, 1)], [YB(kd)])
            it = 0
            for mb in range(2):
                wv, wb = wload([(0, 512, wcols(wout_d[l], mb * 512, 512))], 8, 512)
                for mi in range(4):
                    m = mb * 4 + mi
                    for half in range(2):
                        hs = slice(half * 512, (half + 1) * 512)
                        pbi = it % 2
                        mm_group(ps_t[pbi][:], [(wv[:, kd, mi * 128:(mi + 1) * 128], mbf[:, kd, hs]) for kd in range(8)],
                                 [wb] + [YB(kd) for kd in range(8)], [PB[pbi]])
                        stt(xT[:, m, hs], ps_t[pbi][:], modT[:, l, 16 + m:17 + m], xT[:, m, hs], ALU.mult, ALU.add,
                            [PB[pbi], MOD[l], XB[m][half]], [XB[m][half]])
                        it += 1

        def ffn(l, grp=1):
            tv, tb = temp_phase([("r0", [128, 512], F32), ("r1", [128, 512], F32)])
            it = 0
            for q in range(4):
                hq = hid[q % 2]
                hqb = AR[8 * (q % 2):8 * (q % 2) + 8]
                for j in range(2):
                    wv, wb = wload([(0, 512, wcols(wff1_d[l], q * 1024 + j * 512, 512))], 8, 512)
                    for mi in range(4):
                        fc = j * 4 + mi
                        for half in range(2):
                            hs = slice(half * 512, (half + 1) * 512)
                            pbi = it % 2
                            r = f"r{it % 2}"
                            mm_group(ps_t[pbi][:], [(wv[:, kd, mi * 128:(mi + 1) * 128], hT[:, kd, hs]) for kd in range(8)],
                                     [wb] + [HB[kd][half] for kd in range(8)], [PB[pbi]])
                            act(tv[r], ps_t[pbi][:], AF.Relu, [PB[pbi]], [tb[r]])
                            tt(hq[:, fc, hs], tv[r], tv[r], ALU.mult, [tb[r]], [hqb[fc]])
                            it += 1
                for mb in range(2):
                    wv, wb = wload([(0, 512, wff2_d[l][q * 1024:(q + 1) * 1024, mb * 512:(mb + 1) * 512]
                                     .rearrange("(k p) w -> p k w", p=128))], 8, 512)
                    for mi in range(4):
                        m = mb * 4 + mi
                        for half in range(2):
                            hs = slice(half * 512, (half + 1) * 512)
                            pbi = 2 + it % 2
                            mm_group(ps_t[pbi][:], [(wv[:, fc, mi * 128:(mi + 1) * 128], hq[:, fc, hs]) for fc in range(8)],
                                     [wb] + list(hqb), [PB[pbi]])
                            stt(xT[:, m, hs], ps_t[pbi][:], modT[:, l, 40 + m:41 + m], xT[:, m, hs], ALU.mult, ALU.add,
                                [PB[pbi], MOD[l], XB[m][half]], [XB[m][half]])
                            it += 1

        def load_x(grp):
            tv, tb = temp_phase([("xs0", [128, 1024], F32), ("xs1", [128, 1024], F32)])
            for c in range(NCH):
                b = c % 2
                xs, xsb = tv[f"xs{b}"], tb[f"xs{b}"]
                r0 = grp * TG + c * 128
                S.dma("sp", f"xin{b}", [(xs, x_d[r0:r0 + 128, :])], writes=[xsb])
                for q in range(2):
                    pbi = 2 * b + q
                    for j in range(4):
                        kd = q * 4 + j
                        transpose(ps_t[pbi][:, j * 128:(j + 1) * 128], xs[:, kd * 128:(kd + 1) * 128], identf[:],
                                  [xsb, CONST], [PB[pbi]], inc=(j == 3))
                    cp("act" if q == 0 else "dve", xT[:, q * 4:q * 4 + 4, c * 128:(c + 1) * 128],
                       ps_t[pbi][:].rearrange("p (k t) -> p k t", k=4), [PB[pbi]],
                       [XB[q * 4 + j][c // 4] for j in range(4)])

        def final_store(grp):
            tv, tb = temp_phase([("sq", [128, 8, 512], BF16), ("rt", [128, 512], F32),
                                 ("rstd", [128, 1024], F32), ("os0", [128, 1024], F32), ("os1", [128, 1024], F32)])
            on = merged
            for half in range(2):
                hs = slice(half * 512, (half + 1) * 512)
                for kd in range(8):
                    act(tv["sq"][:, kd, :], xT[:, kd, hs], AF.Square, [XB[kd][half]], [tb["sq"]])
                mm_group(ps_t[5][:], [(onesd[:], tv["sq"][:, kd, :]) for kd in range(8)], [tb["sq"], CONST], [PB[5]])
                act(tv["rt"], ps_t[5][:], AF.Sqrt, [PB[5], CONST], [tb["rt"]], bias=eps_t[:, 0:1], scale=1.0)
                S.op("dve", lambda e, hs=hs: e.reciprocal(tv["rstd"][:, hs], tv["rt"]), reads=[tb["rt"]], writes=[tb["rstd"]])
                for kd in range(8):
                    stt(on[:, kd, hs], xT[:, kd, hs], fnT[:, kd:kd + 1], tv["rstd"][:, hs], ALU.mult, ALU.mult,
                        [XB[kd][half], SMALL, tb["rstd"]], [MB(kd, half)])
            for c in range(NCH):
                b = c % 2
                os_, osb = tv[f"os{b}"], tb[f"os{b}"]
                for q in range(2):
                    pbi = 2 * b + q
                    for j in range(4):
                        kd = q * 4 + j
                        transpose(ps_t[pbi][:, j * 128:(j + 1) * 128], on[:, kd, c * 128:(c + 1) * 128], identf[:],
                                  [MB(kd, c // 4), CONST], [PB[pbi]], inc=(j == 3))
                    cp("act" if q == 0 else "dve", os_[:, q * 512:(q + 1) * 512], ps_t[pbi][:], [PB[pbi]], [osb])
                r0 = grp * TG + c * 128
                S.dma("sp", f"xout{b}", [(out_d[r0:r0 + 128, :], os_)], reads=[osb])
            return [tb["os0"], tb["os1"]]

        import os
        stage = os.environ.get("K_STAGE", "")
        order = ["load", "mod", "norm", "sgu", "br1", "ret", "br0", "pool", "br2", "out", "norm2", "ffn"]
        lim = order.index(stage) if stage in order else 99
        ngrp = 1 if stage else NG
        nlay = 1 if stage else 2
        if lim >= 1:
            compute_mod(0)
        last_out = []
        for grp in range(ngrp):
            load_x(grp)
            for l in range(nlay):
                if grp == 0 and l == 1:
                    compute_mod(1)
                if lim >= 2:
                    norm_to_h(lambda kd, l=l: g12[:, l, 0, kd:kd + 1], lambda kd, l=l: modT[:, l, kd:kd + 1], MOD[l])
                if lim >= 3:
                    retention(l, grp)
                if lim >= 4:
                    branch(l, 0, True)
                if lim >= 5:
                    sgu(l)
                if lim >= 6:
                    branch(l, 1, False)
                if lim >= 7:
                    pool(l, grp)
                if lim >= 8:
                    branch(l, 2, False)
                if lim >= 9:
                    out_proj(l)
                if lim >= 10:
                    norm_to_h(lambda kd, l=l: g12[:, l, 1, kd:kd + 1], lambda kd, l=l: modT[:, l, 24 + kd:25 + kd], MOD[l])
                if lim >= 11:
                    ffn(l, grp)
            last_out = final_store(grp)
        S.final_wait("sp", last_out)
        S.emit()
    return nc


_CACHE = {}


def _perm_w_in(w):
    idx = []
    for h in range(4):
        idx += list(range(OFF_Q + 128 * h, OFF_Q + 128 * h + 128))
        idx += list(range(OFF_K + 128 * h, OFF_K + 128 * h + 128))
        idx += list(range(OFF_V + 256 * h, OFF_V + 256 * h + 256))
    idx += list(range(2048, DIN))
    return np.ascontiguousarray(w[:, :, np.asarray(idx)])


def _prep_inputs(x, c, positions, w_ada, b_ada, norm1, norm2, w_in, ws_gmlp, bs_gmlp, vnorm_gmlp,
                 w_pool, b_pool, pool_scale, w_branch, w_out, w_ff1, w_ff2, final_norm):
    f = lambda a: np.ascontiguousarray(np.asarray(a, dtype=np.float32))
    K = _consts()

    def colT(v, n):
        v = f(v)
        lead = v.shape[:-1]
        return np.ascontiguousarray(np.moveaxis(v.reshape(lead + (n, 128)), -1, 0))

    shared = {
        "w_ada": f(w_ada), "b_adaT": colT(b_ada, 48), "norm1T": colT(norm1, 8), "norm2T": colT(norm2, 8),
        "fnT": colT(final_norm, 8), "w_in": _perm_w_in(f(w_in)), "ws_gmlp": f(ws_gmlp),
        "bsT": np.ascontiguousarray(np.transpose(f(bs_gmlp), (2, 0, 1))),
        "vg_bc": np.ascontiguousarray(np.broadcast_to(f(vnorm_gmlp)[:, None, :], (2, 128, 1024))),
        "w_pool": f(w_pool), "b_poolT": colT(f(b_pool).reshape(2, 1024), 8),
        "pscaleT": colT(pool_scale, 8), "w_branch": f(w_branch), "w_out": f(w_out),
        "w_ff1": f(w_ff1), "w_ff2": f(w_ff2),
        "maskT": K["maskT"], "dq": K["dq"], "dk": K["dk"], "invf": K["invf"], "tril": K["tril"],
        "identf": K["identf"], "identb": K["identb"], "onesd": K["onesd"], "poolm": K["poolm"], "invc": K["invc"],
    }
    x = f(x)
    c = f(c)
    pos = np.asarray(positions).astype(np.int32)
    in_maps = []
    for b in range(NB):
        m = dict(shared)
        m["x"] = np.ascontiguousarray(x[b])
        m["c_col"] = np.ascontiguousarray(c[b].reshape(8, 128).T)
        m["pos"] = np.ascontiguousarray(pos[b].reshape(16, 128).T)
        in_maps.append(m)
    return in_maps


def kernel(**inputs):
    if "nc" not in _CACHE:
        _CACHE["nc"] = build_program()
    nc = _CACHE["nc"]
    in_maps = _prep_inputs(**inputs)
    res = run_bass_kernel_spmd(nc, in_maps, core_ids=list(range(NB)))
    out = np.stack([np.asarray(r["out"], dtype=np.float32) for r in res.results], axis=0)
    return out
```
